# Optimizing a Trainium2 kernel written in Bass

```python
import math
import jax, jax.numpy as jnp
from jax import lax
import numpy as np

D_MODEL = 1024
BATCH = 8
SEQ = 4096
DEPTH = 1

PLE_DIM = 256
EPS = 1e-6
S5_WIDTH = D_MODEL
S5_GROUP = 16
S5_GROUPS = S5_WIDTH // S5_GROUP
S5_STATE = 64
DT_MIN = 1e-3
DT_MAX = 1e-1
LRU_WIDTH = D_MODEL
LRU_HEADS = max(4, LRU_WIDTH // 256)
LRU_BLOCK = LRU_WIDTH // LRU_HEADS
CONV_WIDTH = 4
CONV_PAD = (2, 1)
LRU_C = 8.0
FFN_HIDDEN = -(-8 * D_MODEL // (3 * 256)) * 256
IN_COLS = S5_WIDTH + 2 * LRU_WIDTH + 2 * D_MODEL
SPLITS = [S5_WIDTH, S5_WIDTH + LRU_WIDTH, S5_WIDTH + 2 * LRU_WIDTH, S5_WIDTH + 2 * LRU_WIDTH + D_MODEL]

kernel_name = "hybrid_s5_rglru_gated_encoder_block"


def rms_norm(x, g):
    x32 = x.astype(jnp.float32)
    y = x32 * lax.rsqrt(jnp.mean(x32 * x32, axis=-1, keepdims=True) + EPS)
    return (y * g.astype(jnp.float32)).astype(x.dtype)


def linear_scan(a, b, reverse):
    def combine(c1, c2):
        a1, b1 = c1
        a2, b2 = c2
        return a1 * a2, a2 * b1 + b2
    return lax.associative_scan(combine, (a, b), reverse=reverse, axis=1)[1]


def s5_direction(u, lam_re, lam_im, log_dt, b_re, b_im, c_re, c_im, reverse):
    lam = lax.complex(lam_re, lam_im)
    dt = jnp.exp(log_dt)[:, None]
    lam_bar = jnp.exp(lam * dt)
    b_bar = ((lam_bar - 1.0) / lam)[..., None] * lax.complex(b_re, b_im)
    bu = jnp.einsum('bsgh,gph->bsgp', u.astype(jnp.complex64), b_bar)
    a = jnp.broadcast_to(lam_bar, (1, u.shape[1]) + lam_bar.shape)
    state = linear_scan(a, bu, reverse)
    c = lax.complex(c_re, c_im)
    return jnp.einsum('bsgp,ghp->bsgh', state, c).real


def centred_depthwise_conv(x, w, b):
    out = lax.conv_general_dilated(x, w[:, None, :], window_strides=(1,), padding=[CONV_PAD],
                                   dimension_numbers=('NWC', 'WIO', 'NWC'),
                                   feature_group_count=x.shape[-1])
    return out + b


def rglru_direction(xc, wa, ba, wx, bx, a_logit, reverse):
    bsz, slen, width = xc.shape
    xh = xc.reshape(bsz, slen, LRU_HEADS, LRU_BLOCK)
    r = jax.nn.sigmoid(jnp.einsum('bshi,hij->bshj', xh, wa).reshape(bsz, slen, width) + ba)
    i = jax.nn.sigmoid(jnp.einsum('bshi,hij->bshj', xh, wx).reshape(bsz, slen, width) + bx)
    log_a = -LRU_C * r * jax.nn.softplus(-a_logit)
    a = jnp.exp(log_a)
    mult = jnp.sqrt(-jnp.expm1(2.0 * log_a))
    return linear_scan(a, mult * (i * xc), reverse)


def setup_inputs(seed: int = 0) -> dict:
    key = jax.random.key(seed)
    ks = iter(jax.random.split(key, 40))
    f32 = jnp.float32

    def nrm(shape, scale):
        return jax.random.normal(next(ks), shape, f32) * scale

    L, G, P, H = DEPTH, S5_GROUPS, S5_STATE, S5_GROUP
    x = nrm((BATCH, SEQ, D_MODEL), 1.0)
    p = nrm((L, BATCH, SEQ, PLE_DIM), 1.0)
    g_mix = 1.0 + nrm((L, D_MODEL), 0.01)
    w_in = nrm((L, D_MODEL, IN_COLS), D_MODEL ** -0.5)
    n = jnp.arange(P, dtype=f32)
    s5_lambda_re = -0.5 + nrm((L, 2, G, P), 0.01)
    s5_lambda_im = math.pi * n + nrm((L, 2, G, P), 0.01)
    s5_log_dt = jax.random.uniform(next(ks), (L, 2, G), f32, math.log(DT_MIN), math.log(DT_MAX))
    s5_b_re = nrm((L, 2, G, P, H), (2.0 * H) ** -0.5)
    s5_b_im = nrm((L, 2, G, P, H), (2.0 * H) ** -0.5)
    s5_c_re = nrm((L, 2, G, H, P), P ** -0.5)
    s5_c_im = nrm((L, 2, G, H, P), P ** -0.5)
    s5_d = nrm((L, S5_WIDTH), 1.0)
    s5_glu_w = nrm((L, S5_WIDTH, S5_WIDTH), S5_WIDTH ** -0.5)
    s5_glu_b = nrm((L, S5_WIDTH), 0.01)
    w_a_out = nrm((L, S5_WIDTH, D_MODEL), S5_WIDTH ** -0.5)
    lru_conv_w = nrm((L, CONV_WIDTH, LRU_WIDTH), CONV_WIDTH ** -0.5)
    lru_conv_b = nrm((L, LRU_WIDTH), 0.01)
    lru_wa = nrm((L, 2, LRU_HEADS, LRU_BLOCK, LRU_BLOCK), LRU_BLOCK ** -0.5)
    lru_ba = nrm((L, 2, LRU_WIDTH), 0.01)
    lru_wx = nrm((L, 2, LRU_HEADS, LRU_BLOCK, LRU_BLOCK), LRU_BLOCK ** -0.5)
    lru_bx = nrm((L, 2, LRU_WIDTH), 0.01)
    rad = jax.random.uniform(next(ks), (L, 2, LRU_WIDTH), f32, 0.9, 0.999)
    s = rad ** (1.0 / LRU_C)
    lru_a_logit = jnp.log(s) - jnp.log1p(-s)
    w_b_out = nrm((L, LRU_WIDTH, D_MODEL), LRU_WIDTH ** -0.5)
    w_o = nrm((L, D_MODEL, D_MODEL), D_MODEL ** -0.5)
    g_ffn = 1.0 + nrm((L, D_MODEL), 0.01)
    w_ff_gate = nrm((L, D_MODEL, FFN_HIDDEN), D_MODEL ** -0.5)
    w_ff_up = nrm((L, D_MODEL, FFN_HIDDEN), D_MODEL ** -0.5)
    w_ff_down = nrm((L, FFN_HIDDEN, D_MODEL), FFN_HIDDEN ** -0.5)
    g_ple = 1.0 + nrm((L, D_MODEL), 0.01)
    w_ple_gate = nrm((L, D_MODEL, D_MODEL), D_MODEL ** -0.5)
    w_ple_proj = nrm((L, PLE_DIM, D_MODEL), PLE_DIM ** -0.5)
    g_final = 1.0 + nrm((D_MODEL,), 0.01)
    return {"x": x, "p": p, "g_mix": g_mix, "w_in": w_in,
            "s5_lambda_re": s5_lambda_re, "s5_lambda_im": s5_lambda_im, "s5_log_dt": s5_log_dt,
            "s5_b_re": s5_b_re, "s5_b_im": s5_b_im, "s5_c_re": s5_c_re, "s5_c_im": s5_c_im,
            "s5_d": s5_d, "s5_glu_w": s5_glu_w, "s5_glu_b": s5_glu_b, "w_a_out": w_a_out,
            "lru_conv_w": lru_conv_w, "lru_conv_b": lru_conv_b, "lru_wa": lru_wa, "lru_ba": lru_ba,
            "lru_wx": lru_wx, "lru_bx": lru_bx, "lru_a_logit": lru_a_logit, "w_b_out": w_b_out,
            "w_o": w_o, "g_ffn": g_ffn, "w_ff_gate": w_ff_gate, "w_ff_up": w_ff_up,
            "w_ff_down": w_ff_down, "g_ple": g_ple, "w_ple_gate": w_ple_gate,
            "w_ple_proj": w_ple_proj, "g_final": g_final}


def reference(x, p, g_mix, w_in, s5_lambda_re, s5_lambda_im, s5_log_dt, s5_b_re, s5_b_im,
              s5_c_re, s5_c_im, s5_d, s5_glu_w, s5_glu_b, w_a_out, lru_conv_w, lru_conv_b,
              lru_wa, lru_ba, lru_wx, lru_bx, lru_a_logit, w_b_out, w_o, g_ffn, w_ff_gate,
              w_ff_up, w_ff_down, g_ple, w_ple_gate, w_ple_proj, g_final):
    f32 = jnp.float32
    dtype = x.dtype
    bsz, slen, _ = x.shape
    h = x
    for l in range(DEPTH):
        u = rms_norm(h, g_mix[l])
        proj = u @ w_in[l]
        u_a, x_b, gate_b, merge_a, merge_b = jnp.split(proj, SPLITS, axis=-1)

        ua = u_a.astype(f32)
        ua_g = ua.reshape(bsz, slen, S5_GROUPS, S5_GROUP)
        y_a = ua * s5_d[l].astype(f32)
        for d in range(2):
            y_a = y_a + s5_direction(
                ua_g, s5_lambda_re[l, d].astype(f32), s5_lambda_im[l, d].astype(f32),
                s5_log_dt[l, d].astype(f32), s5_b_re[l, d].astype(f32), s5_b_im[l, d].astype(f32),
                s5_c_re[l, d].astype(f32), s5_c_im[l, d].astype(f32), reverse=(d == 1),
            ).reshape(bsz, slen, S5_WIDTH)
        g_a = jax.nn.gelu(y_a)
        y_a = g_a * jax.nn.sigmoid(g_a @ s5_glu_w[l].astype(f32) + s5_glu_b[l].astype(f32))
        branch_a = y_a.astype(dtype) @ w_a_out[l]

        xc = centred_depthwise_conv(x_b.astype(f32), lru_conv_w[l].astype(f32), lru_conv_b[l].astype(f32))
        h_b = jnp.zeros_like(xc)
        for d in range(2):
            h_b = h_b + rglru_direction(
                xc, lru_wa[l, d].astype(f32), lru_ba[l, d].astype(f32), lru_wx[l, d].astype(f32),
                lru_bx[l, d].astype(f32), lru_a_logit[l, d].astype(f32), reverse=(d == 1))
        y_b = h_b * jax.nn.gelu(gate_b.astype(f32))
        branch_b = y_b.astype(dtype) @ w_b_out[l]

        merged = jax.nn.sigmoid(merge_a) * branch_a + jax.nn.sigmoid(merge_b) * branch_b
        h = h + merged @ w_o[l]

        v = rms_norm(h, g_ffn[l])
        h = h + (jax.nn.silu(v @ w_ff_gate[l]) * (v @ w_ff_up[l])) @ w_ff_down[l]

        gate_p = jax.nn.sigmoid(rms_norm(h, g_ple[l]) @ w_ple_gate[l])
        h = h + gate_p * (p[l] @ w_ple_proj[l])
    return rms_norm(h, g_final)
```

```python
import math
import numpy as np
from contextlib import ExitStack
import concourse.bass as bass
import concourse.mybir as mybir
from concourse.bass_utils import run_bass_kernel_spmd

F32 = mybir.dt.float32
BF16 = mybir.dt.bfloat16
I32 = mybir.dt.int32
AF = mybir.ActivationFunctionType
ALU = mybir.AluOpType
AX = mybir.AxisListType

S = 4096
D = 1024
NCORE = 8
FH = 2816
ENGS = ("pe", "act", "dve", "pool", "sp")
NLANES = 24
TWO_PI = 2.0 * math.pi


class Prog:
    def __init__(self, nc, plan, info=None):
        self.nc = nc
        self.plan = plan
        self.idx = 0
        self.res = {}
        self.clock = {}
        self.eclock = {e: {} for e in ENGS}
        self.stream_of = {}
        self.last = {}
        self.lane_rr = 0
        if plan:
            self.signal = set()
        else:
            self.signal = info
            self.cnt = {}
            self.sigval = {}
            self.sems = {}
            self.engh = {"pe": nc.tensor, "act": nc.scalar, "dve": nc.vector,
                         "pool": nc.gpsimd, "sp": nc.sync}

    def sem(self, stream):
        if stream not in self.sems:
            self.sems[stream] = self.nc._semctx.enter_context(self.nc.semaphore("s_" + stream))
        return self.sems[stream]

    def _conf(self, key):
        d = self.res.setdefault(key[0], {})
        out = []
        for k, v in d.items():
            n = min(len(k), len(key))
            if k[:n] == key[:n]:
                out.append((k, v))
        return d, out

    def op(self, eng, reads, writes, emit, dma=False):
        if getattr(self, "mute", False) or getattr(self, "dead", False):
            return -1
        i = self.idx
        self.idx += 1
        if dma:
            lane = self.lane_rr % NLANES
            self.lane_rr += 1
            stream = "lane%d" % lane
            writes = list(writes) + [("__lane", lane)]
        else:
            stream = eng
        self.stream_of[i] = stream
        deps = set()
        for key in reads:
            key = key if isinstance(key, tuple) else (key,)
            d, confs = self._conf(key)
            for k, v in confs:
                if v[0] is not None:
                    deps.add(v[0])
            d.setdefault(key, [None, []])[1].append(i)
        for key in writes:
            key = key if isinstance(key, tuple) else (key,)
            d, confs = self._conf(key)
            for k, v in confs:
                if v[0] is not None:
                    deps.add(v[0])
                deps.update(v[1])
                del d[k]
            d[key] = [i, []]
        deps.discard(i)
        ec = self.eclock[eng]
        need = []
        for y in sorted(deps, reverse=True):
            sy = self.stream_of[y]
            if sy == "pe" and stream == "pe":
                continue
            if ec.get(sy, -1) >= y:
                continue
            need.append(y)
            for s, v in self.clock[y].items():
                if ec.get(s, -1) < v:
                    ec[s] = v
        if self.plan:
            self.signal.update(need)
            if dma:
                self.signal.add(i)
        else:
            h = self.engh[eng]
            for y in need:
                h.wait_ge(self.sem(self.stream_of[y]), self.sigval[y])
            ins = emit(h)
            if i in self.signal:
                inc = 16 if dma else 1
                c = self.cnt.get(stream, 0) + inc
                self.cnt[stream] = c
                self.sigval[i] = c
                ins.then_inc(self.sem(stream), inc)
        ck = dict(ec)
        ck[stream] = i
        self.clock[i] = ck
        self.last[stream] = i
        return i

    def barrier(self):
        if getattr(self, "mute", False) or getattr(self, "dead", False):
            return
        lasts = dict(self.last)
        for eng in ENGS:
            ec = self.eclock[eng]
            for stream, y in lasts.items():
                if stream == eng and eng == "pe":
                    continue
                if ec.get(stream, -1) >= y:
                    continue
                if self.plan:
                    self.signal.add(y)
                else:
                    self.engh[eng].wait_ge(self.sem(stream), self.sigval[y])
                for s_, v in self.clock[y].items():
                    if ec.get(s_, -1) < v:
                        ec[s_] = v

    def final_wait(self, eng, ops):
        ops = [o for o in ops if o >= 0]
        if self.plan:
            self.signal.update(ops)
            return
        h = self.engh[eng]
        for y in ops:
            h.wait_ge(self.sem(self.stream_of[y]), self.sigval[y])


class _Stop(Exception):
    pass


import os
KSTOP = int(os.environ.get("KSTOP", "99"))


def build(nc, P, T):
    top = nc._ctx

    def sbt(ctx, name, shape, dt):
        return ctx.enter_context(nc.sbuf_tensor("sb_" + name, shape, dt))

    pb = [top.enter_context(nc.psum_tensor("pb%d" % i, [128, 512], F32)) for i in range(6)]
    pt = [top.enter_context(nc.psum_tensor("pt%d" % i, [128, 1024], BF16)) for i in range(2)]
    cnt = {"pb": 0, "pt": 0, "stg": 0}

    def next_pb():
        i = cnt["pb"] % 6
        cnt["pb"] += 1
        return pb[i], ("pb", i)

    def next_pt():
        i = cnt["pt"] % 2
        cnt["pt"] += 1
        return pt[i], ("pt", i)

    ident = sbt(top, "ident", [128, 128], BF16)
    identf = sbt(top, "identf", [128, 128], F32)
    eps = sbt(top, "eps", [128, 1], F32)
    ones = sbt(top, "ones", [128, 1], F32)
    halfpi = sbt(top, "halfpi", [128, 1], F32)
    gcols = sbt(top, "gcols", [128, 3, 8], F32)
    stage = [sbt(top, "stage%d" % i, [128, 1024], F32) for i in range(2)]

    P.op("sp", [], ["identf"], lambda e: e.dma_start(out=identf[:], in_=T["identf"][:, :]), dma=True)
    P.op("dve", ["identf"], ["ident"], lambda e: e.tensor_copy(out=ident[:], in_=identf[:]))
    P.op("dve", [], ["eps"], lambda e: e.memset(eps[:], 1e-6))
    P.op("dve", [], ["ones"], lambda e: e.memset(ones[:], 1.0))
    P.op("dve", [], ["halfpi"], lambda e: e.memset(halfpi[:], math.pi / 2))
    P.op("sp", [], ["gcols"], lambda e: e.dma_start(out=gcols[:], in_=T["gcols"][:, :, :]), dma=True)

    def load_cast(dst, dkey, src, K, N):
        for k in range(K):
            for c0 in range(0, N, 1024):
                cw = min(1024, N - c0)
                si = cnt["stg"] % 2
                cnt["stg"] += 1
                st = stage[si]
                P.op("sp", [], [("stage", si)],
                     lambda e, st=st, k=k, c0=c0, cw=cw: e.dma_start(out=st[:, 0:cw], in_=src[k * 128:(k + 1) * 128, c0:c0 + cw]), dma=True)
                P.op("pool", [("stage", si)], [dkey + (k, c0)],
                     lambda e, st=st, k=k, c0=c0, cw=cw: e.tensor_copy(out=dst[:, k, c0:c0 + cw], in_=st[:, 0:cw]))

    def mm_acc(out_ap, okey, terms):
        n = len(terms)
        for i, (l, r, rk) in enumerate(terms):
            P.op("pe", rk, [okey],
                 lambda e, l=l, r=r, i=i: e.matmul(out=out_ap, lhsT=l, rhs=r, start=(i == 0), stop=(i == n - 1)))

    def norm_T(src, skey, gi, dst, dkey, tmp):
        junk, ss, rs, rr, xn = tmp
        P.op("act", [skey], ["junk"], lambda e: e.activation(out=junk[:], in_=src, func=AF.Square))
        P.op("dve", ["junk"], ["ss"], lambda e: e.tensor_reduce(out=ss[:], in_=junk[:], axis=AX.X, op=ALU.add))
        P.op("act", ["ss", "eps"], ["rs"],
             lambda e: e.activation(out=rs[:], in_=ss[:], func=AF.Sqrt, scale=1.0 / D, bias=eps[:]))
        P.op("dve", ["rs"], ["rr"], lambda e: e.reciprocal(out=rr[:], in_=rs[:]))
        P.op("act", [skey, "rr"], ["xn"], lambda e: e.activation(out=xn[:], in_=src, func=AF.Copy, scale=rr[:, 0:1]))
        ptile, pkey = next_pt()
        for k in range(8):
            P.op("pe", ["xn", "ident"], [pkey],
                 lambda e, k=k: e.transpose(out=ptile[:, k * 128:(k + 1) * 128], in_=xn[:, k * 128:(k + 1) * 128],
                                            identity=ident[:]))
        if gi is None:
            P.op("act", [pkey], [dkey], lambda e: e.copy(out=dst, in_=ptile[:].rearrange("p (k t) -> p k t", k=8)))
        else:
            P.op("dve", [pkey, "gcols"], [dkey],
                 lambda e: e.tensor_tensor(out=dst, in0=ptile[:].rearrange("p (k t) -> p k t", k=8),
                                           in1=gcols[:, gi, :].unsqueeze(2).to_broadcast([128, 8, 128]), op=ALU.mult))
        return rr

    ntmp = (sbt(top, "junk", [128, D], F32), sbt(top, "ss", [128, 1], F32), sbt(top, "rs", [128, 1], F32),
            sbt(top, "rr", [128, 1], F32), sbt(top, "xn", [128, D], BF16))

    xd = T["x"]
    yd = T["y"]
    ybs = T["yb_scr"]

    uas = T["ua_scr"]
    KSKIP = int(os.environ.get("KSKIP", "0"))
    with ExitStack() as c13:
        P.mute = bool(KSKIP)
        unT = sbt(c13, "unT", [128, 8, S], BF16)
        with ExitStack() as c1:
            xt2 = [sbt(c1, "xt%d" % i, [128, D], F32) for i in range(2)]
            for tb in range(32):
                xt = xt2[tb % 2]
                P.op("sp", [], [("xt", tb % 2)],
                     lambda e, xt=xt, tb=tb: e.dma_start(out=xt[:], in_=xd[tb * 128:(tb + 1) * 128, :]), dma=True)
                norm_T(xt[:], ("xt", tb % 2), 0, unT[:, :, tb * 128:(tb + 1) * 128], ("unT", tb // 4, tb % 4), ntmp)

        P.barrier()
        if KSTOP <= 1:
            P.dead = True
        with ExitStack() as c3a:
            wua = sbt(c3a, "wua", [128, 8, 128], BF16)
            uaS = sbt(c3a, "uaS", [128, 8, 512], BF16)
            for ft in range(8):
                load_cast(wua, ("wua",), T["w_in"][:, ft * 128:(ft + 1) * 128], 8, 128)
                for tt in range(8):
                    pu, pku = next_pb()
                    mm_acc(pu[:], pku, [(wua[:, k, :], unT[:, k, tt * 512:(tt + 1) * 512], [("wua", k), ("unT", tt)]) for k in range(8)])
                    P.op("act", [pku], [("uaS", tt)],
                         lambda e, pu=pu, tt=tt: e.copy(out=uaS[:, :, tt * 64:(tt + 1) * 64],
                                                        in_=pu[:].rearrange("p (c j) -> p j c", j=8)))
                P.op("sp", ["uaS"], [("uas", ft)], lambda e, ft=ft: e.dma_start(out=uas[:, ft, :, :], in_=uaS[:]), dma=True)

        P.barrier()
        if KSTOP <= 2:
            P.dead = True
        with ExitStack() as c2:
            lsm = sbt(c2, "lsm", [128, 8, 16], F32)
            P.op("sp", [], ["lsm"], lambda e: e.dma_start(out=lsm[:], in_=T["lru_small"][:, :, :]), dma=True)
            cneg = sbt(c2, "cneg", [128, 8, 2], F32)
            cn2 = sbt(c2, "cn2", [128, 8, 2], F32)
            et = sbt(c2, "et", [128, 8, 2], F32)
            P.op("act", ["lsm"], ["et"], lambda e: e.activation(out=et[:], in_=lsm[:, :, 9:11], func=AF.Exp, scale=-1.0))
            P.op("act", ["et", "ones"], ["cn2"], lambda e: e.activation(out=cn2[:], in_=et[:], func=AF.Ln, bias=ones[:]))
            P.op("dve", ["cn2"], ["cneg"], lambda e: e.tensor_scalar(out=cneg[:], in0=cn2[:], scalar1=-8.0, scalar2=None, op0=ALU.mult))
            P.op("dve", ["cneg"], ["cn2"], lambda e: e.tensor_scalar(out=cn2[:], in0=cneg[:], scalar1=2.0, scalar2=None, op0=ALU.mult))

            wxb = sbt(c2, "wxb", [128, 8, 256], BF16)
            wgb = sbt(c2, "wgb", [128, 8, 256], BF16)
            wgt = sbt(c2, "wgt", [128, 2, 2, 2, 256], BF16)
            xb = sbt(c2, "xb", [128, S + 4], F32)
            xc = sbt(c2, "xc", [128, 2, S], F32)
            xcb = sbt(c2, "xcb", [128, 2, S], BF16)
            hb = sbt(c2, "hb", [128, S], F32)
            Q = 1024
            ra = sbt(c2, "ra", [128, Q], F32)
            ri = sbt(c2, "ri", [128, Q], F32)
            rm = sbt(c2, "rm", [128, Q], F32)
            gtm = sbt(c2, "gtm", [128, 512], F32)
            carry = sbt(c2, "carry", [128, 1], F32)
            ybt = sbt(c2, "ybt", [128, S], BF16)
            P.op("dve", [], ["xb"], lambda e: e.memset(xb[:], 0.0))
            for hdx in range(4):
                load_cast(wxb, ("wxb",), T["w_in"][:, D + hdx * 256: D + (hdx + 1) * 256], 8, 256)
                load_cast(wgb, ("wgb",), T["w_in"][:, 2 * D + hdx * 256: 2 * D + (hdx + 1) * 256], 8, 256)
                for d in range(2):
                    for gt, nm in enumerate(("lru_wa", "lru_wx")):
                        for kin in range(2):
                            si = cnt["stg"] % 2
                            cnt["stg"] += 1
                            st = stage[si]
                            P.op("sp", [], [("stage", si)],
                                 lambda e, st=st, d=d, nm=nm, kin=kin, hdx=hdx: e.dma_start(
                                     out=st[:, 0:256], in_=T[nm][d, hdx, kin * 128:(kin + 1) * 128, :]), dma=True)
                            P.op("pool", [("stage", si)], [("wgt", d, gt, kin)],
                                 lambda e, st=st, d=d, gt=gt, kin=kin: e.tensor_copy(out=wgt[:, d, gt, kin, :], in_=st[:, 0:256]))
                for m in range(2):
                    ft = hdx * 2 + m
                    for tt in range(8):
                        pbk, pk = next_pb()
                        mm_acc(pbk[:], pk, [(wxb[:, k, m * 128:(m + 1) * 128], unT[:, k, tt * 512:(tt + 1) * 512],
                                             [("wxb", k), ("unT", tt)]) for k in range(8)])
                        P.op("act", [pk], [("xb", tt)],
                             lambda e, pbk=pbk, tt=tt: e.copy(out=xb[:, 2 + tt * 512: 2 + (tt + 1) * 512], in_=pbk[:]))
                    P.op("act", ["xb", "lsm"], [("xc", m)],
                         lambda e, m=m, ft=ft: e.activation(out=xc[:, m, :], in_=xb[:, 0:S], func=AF.Identity,
                                                            scale=lsm[:, ft, 0:1], bias=lsm[:, ft, 4:5]))
                    for k in range(1, 4):
                        P.op("dve", ["xb", ("xc", m), "lsm"], [("xc", m)],
                             lambda e, m=m, ft=ft, k=k: e.scalar_tensor_tensor(
                                 out=xc[:, m, :], in0=xb[:, k:k + S], scalar=lsm[:, ft, k:k + 1], in1=xc[:, m, :],
                                 op0=ALU.mult, op1=ALU.add))
                    P.op("act", [("xc", m)], [("xcb", m)], lambda e, m=m: e.copy(out=xcb[:, m, :], in_=xc[:, m, :]))
                for m in range(2):
                    ft = hdx * 2 + m
                    for d in range(2):
                        for q in (range(4) if d == 0 else range(3, -1, -1)):
                            qs = slice(q * Q, (q + 1) * Q)
                            for t2 in range(2):
                                sl = slice(q * Q + t2 * 512, q * Q + (t2 + 1) * 512)
                                ls_ = slice(t2 * 512, (t2 + 1) * 512)
                                pa, pka = next_pb()
                                mm_acc(pa[:], pka, [(wgt[:, d, 0, kin, m * 128:(m + 1) * 128], xcb[:, kin, sl],
                                                     [("wgt", d, 0, kin), ("xcb", kin)]) for kin in range(2)])
                                P.op("act", [pka, "lsm"], [("ra", t2)],
                                     lambda e, pa=pa, ls_=ls_, ft=ft, d=d: e.activation(out=ra[:, ls_], in_=pa[:], func=AF.Sigmoid,
                                                                                       bias=lsm[:, ft, 5 + d:6 + d]))
                                px, pkx = next_pb()
                                mm_acc(px[:], pkx, [(wgt[:, d, 1, kin, m * 128:(m + 1) * 128], xcb[:, kin, sl],
                                                     [("wgt", d, 1, kin), ("xcb", kin)]) for kin in range(2)])
                                P.op("act", [pkx, "lsm"], [("ri", t2)],
                                     lambda e, px=px, ls_=ls_, ft=ft, d=d: e.activation(out=ri[:, ls_], in_=px[:], func=AF.Sigmoid,
                                                                                       bias=lsm[:, ft, 7 + d:8 + d]))
                            P.op("act", ["ra", "cn2"], ["rm"],
                                 lambda e, ft=ft, d=d: e.activation(out=rm[:], in_=ra[:], func=AF.Exp, scale=cn2[:, ft, d:d + 1]))
                            P.op("act", ["rm", "ones"], ["rm"],
                                 lambda e: e.activation(out=rm[:], in_=rm[:], func=AF.Sqrt, scale=-1.0, bias=ones[:]))
                            P.op("act", ["ra", "cneg"], ["ra"],
                                 lambda e, ft=ft, d=d: e.activation(out=ra[:], in_=ra[:], func=AF.Exp, scale=cneg[:, ft, d:d + 1]))
                            P.op("pool", ["ri", ("xc", m)], ["ri"],
                                 lambda e, m=m, qs=qs: e.tensor_tensor(out=ri[:], in0=ri[:], in1=xc[:, m, qs], op=ALU.mult))
                            P.op("dve", ["ri", "rm"], ["ri"], lambda e: e.tensor_tensor(out=ri[:], in0=ri[:], in1=rm[:], op=ALU.mult))
                            if d == 0:
                                ini = 0.0 if q == 0 else hb[:, q * Q - 1:q * Q]
                                P.op("dve", ["ra", "ri", "hb"], ["hb"],
                                     lambda e, qs=qs, ini=ini: e.tensor_tensor_scan(out=hb[:, qs], data0=ra[:], data1=ri[:], initial=ini,
                                                                                    op0=ALU.mult, op1=ALU.add))
                            else:
                                ini = 0.0 if q == 3 else carry[:, 0:1]
                                P.op("dve", ["ra", "ri", "carry"], ["rm"],
                                     lambda e, ini=ini: e.tensor_tensor_scan(out=rm[:, ::-1], data0=ra[:, ::-1], data1=ri[:, ::-1],
                                                                             initial=ini, op0=ALU.mult, op1=ALU.add))
                                P.op("act", ["rm"], ["carry"], lambda e: e.copy(out=carry[:], in_=rm[:, 0:1]))
                                P.op("pool", ["hb", "rm"], ["hb"], lambda e, qs=qs: e.tensor_tensor(out=hb[:, qs], in0=hb[:, qs], in1=rm[:], op=ALU.add))
                    for tt in range(8):
                        sl = slice(tt * 512, (tt + 1) * 512)
                        pg, pkg = next_pb()
                        mm_acc(pg[:], pkg, [(wgb[:, k, m * 128:(m + 1) * 128], unT[:, k, sl], [("wgb", k), ("unT", tt)])
                                            for k in range(8)])
                        P.op("act", [pkg], ["gtm"], lambda e, pg=pg: e.activation(out=gtm[:], in_=pg[:], func=AF.Gelu_apprx_tanh))
                        P.op("dve", ["hb", "gtm"], [("ybt", tt)], lambda e, sl=sl: e.tensor_tensor(out=ybt[:, sl], in0=hb[:, sl], in1=gtm[:], op=ALU.mult))
                    P.op("sp", ["ybt"], [("ybs", ft)], lambda e, ft=ft: e.dma_start(out=ybs[:, ft, :], in_=ybt[:]), dma=True)

    P.mute = False
    P.barrier()
    if KSTOP <= 3:
        P.dead = True
    c34 = ExitStack()
    nc._late = c34
    gaT = sbt(c34, "gaT", [128, 8, S], BF16)
    if True:
        with ExitStack() as c3:
            sp_ = {}
            for nm in ("lamre", "lamim", "logdt"):
                sp_[nm] = sbt(c3, "s5" + nm, [128, 64], F32)
                P.op("sp", [], [nm], lambda e, nm=nm: e.dma_start(out=sp_[nm][:], in_=T["s5_" + nm][:, :]), dma=True)
            dcol = sbt(c3, "dcol", [128, 8], F32)
            P.op("sp", [], ["dcol"], lambda e: e.dma_start(out=dcol[:], in_=T["s5_dcol"][:, :]), dma=True)
            maskf = sbt(c3, "maskf", [128, 128], F32)
            maskb = sbt(c3, "maskb", [128, 128], F32)
            kidx = sbt(c3, "kidx", [128, 512], F32)
            tauin = sbt(c3, "tauin", [128, 8], F32)
            tauout = sbt(c3, "tauout", [128, 8], F32)
            P.op("sp", [], ["maskf"], lambda e: e.dma_start(out=maskf[:], in_=T["maskf"][:, :]), dma=True)
            P.op("sp", [], ["maskb"], lambda e: e.dma_start(out=maskb[:], in_=T["maskb"][:, :]), dma=True)
            P.op("sp", [], ["kidx"], lambda e: e.dma_start(out=kidx[:], in_=T["kidx"][:, :]), dma=True)
            P.op("sp", [], ["tauin"], lambda e: e.dma_start(out=tauin[:], in_=T["tauin"][:, :]), dma=True)
            P.op("sp", [], ["tauout"], lambda e: e.dma_start(out=tauout[:], in_=T["tauout"][:, :]), dma=True)

            def s5t(name, shape, dt=F32):
                return sbt(c3, name, shape, dt)

            dt_ = s5t("dt_", [128, 64])
            ar = s5t("ar", [128, 64])
            ai = s5t("ai", [128, 64])
            P.op("act", ["logdt"], ["dt_"], lambda e: e.activation(out=dt_[:], in_=sp_["logdt"][:], func=AF.Exp))
            P.op("dve", ["dt_", "lamre"], ["ar"], lambda e: e.tensor_tensor(out=ar[:], in0=sp_["lamre"][:], in1=dt_[:], op=ALU.mult))
            P.op("dve", ["dt_", "lamim"], ["ai"], lambda e: e.tensor_tensor(out=ai[:], in0=sp_["lamim"][:], in1=dt_[:], op=ALU.mult))

            F_MAX = 512
            ce_k = s5t("ce_k", [128, F_MAX], I32)
            ce_f = s5t("ce_f", [128, F_MAX])
            ce_r = s5t("ce_r", [128, F_MAX])
            ce_a = s5t("ce_a", [128, F_MAX])
            ce_e = s5t("ce_e", [128, F_MAX])

            def cexp(Aap, Akeys, Map, Mkeys, re, im, okeys, F, shape=None):
                def v(t):
                    a = t[:, 0:F]
                    return a if shape is None else a.rearrange(shape[0], **shape[1])
                P.op("dve", Akeys, ["ce_k"], lambda e: e.tensor_scalar(out=v(ce_k), in0=Aap, scalar1=1.0 / TWO_PI, scalar2=None, op0=ALU.mult))
                P.op("dve", ["ce_k"], ["ce_f"], lambda e: e.tensor_copy(out=ce_f[:, 0:F], in_=ce_k[:, 0:F]))
                P.op("dve", ["ce_f"] + Akeys, ["ce_r"],
                     lambda e: e.scalar_tensor_tensor(out=v(ce_r), in0=v(ce_f), scalar=-6.28125, in1=Aap, op0=ALU.mult, op1=ALU.add))
                P.op("dve", ["ce_f", "ce_r"], ["ce_r"],
                     lambda e: e.scalar_tensor_tensor(out=ce_r[:, 0:F], in0=ce_f[:, 0:F], scalar=-(TWO_PI - 6.28125), in1=ce_r[:, 0:F],
                                                      op0=ALU.mult, op1=ALU.add))
                P.op("dve", ["ce_r"], ["ce_r"],
                     lambda e: e.tensor_scalar(out=ce_r[:, 0:F], in0=ce_r[:, 0:F], scalar1=3.14159, scalar2=-3.14159, op0=ALU.min, op1=ALU.max))
                P.op("dve", ["ce_r"], ["ce_a"],
                     lambda e: e.scalar_tensor_tensor(out=ce_a[:, 0:F], in0=ce_r[:, 0:F], scalar=-1.0, in1=ce_r[:, 0:F], op0=ALU.mult, op1=ALU.max))
                if Map is None:
                    P.op("act", ["ce_r"], okeys[1:2], lambda e: e.activation(out=im, in_=v(ce_r), func=AF.Sin))
                    P.op("act", ["ce_a", "halfpi"], okeys[0:1],
                         lambda e: e.activation(out=re, in_=v(ce_a), func=AF.Sin, scale=-1.0, bias=halfpi[:]))
                else:
                    P.op("act", ["ce_r"], ["ce_r"], lambda e: e.activation(out=ce_r[:, 0:F], in_=ce_r[:, 0:F], func=AF.Sin))
                    P.op("act", ["ce_a", "halfpi"], ["ce_a"],
                         lambda e: e.activation(out=ce_a[:, 0:F], in_=ce_a[:, 0:F], func=AF.Sin, scale=-1.0, bias=halfpi[:]))
                    P.op("act", Mkeys, ["ce_e"], lambda e: e.activation(out=v(ce_e), in_=Map, func=AF.Exp))
                    P.op("dve", ["ce_e", "ce_a"], okeys[0:1], lambda e: e.tensor_tensor(out=re, in0=v(ce_e), in1=v(ce_a), op=ALU.mult))
                    P.op("dve", ["ce_e", "ce_r"], okeys[1:2], lambda e: e.tensor_tensor(out=im, in0=v(ce_e), in1=v(ce_r), op=ALU.mult))

            lbr = s5t("lbr", [128, 64]); lbi = s5t("lbi", [128, 64])
            cexp(ai[:], ["ai"], ar[:], ["ar"], lbr[:], lbi[:], ["lbr", "lbi"], 64)
            qr = s5t("qr", [128, 64]); qi = s5t("qi", [128, 64])
            t1 = s5t("t1", [128, 64]); t2 = s5t("t2", [128, 64]); t3 = s5t("t3", [128, 64])
            lre, lim = sp_["lamre"], sp_["lamim"]
            P.op("dve", ["lamre"], ["t1"], lambda e: e.tensor_tensor(out=t1[:], in0=lre[:], in1=lre[:], op=ALU.mult))
            P.op("dve", ["lamim"], ["t2"], lambda e: e.tensor_tensor(out=t2[:], in0=lim[:], in1=lim[:], op=ALU.mult))
            P.op("dve", ["t1", "t2"], ["t1"], lambda e: e.tensor_tensor(out=t1[:], in0=t1[:], in1=t2[:], op=ALU.add))
            P.op("dve", ["t1"], ["t3"], lambda e: e.reciprocal(out=t3[:], in_=t1[:]))
            P.op("dve", ["lbr"], ["t1"], lambda e: e.tensor_scalar(out=t1[:], in0=lbr[:], scalar1=-1.0, scalar2=None, op0=ALU.add))
            P.op("dve", ["t1", "lamre"], ["qr"], lambda e: e.tensor_tensor(out=qr[:], in0=t1[:], in1=lre[:], op=ALU.mult))
            P.op("dve", ["lbi", "lamim"], ["t2"], lambda e: e.tensor_tensor(out=t2[:], in0=lbi[:], in1=lim[:], op=ALU.mult))
            P.op("dve", ["qr", "t2"], ["qr"], lambda e: e.tensor_tensor(out=qr[:], in0=qr[:], in1=t2[:], op=ALU.add))
            P.op("dve", ["qr", "t3"], ["qr"], lambda e: e.tensor_tensor(out=qr[:], in0=qr[:], in1=t3[:], op=ALU.mult))
            P.op("dve", ["lbi", "lamre"], ["qi"], lambda e: e.tensor_tensor(out=qi[:], in0=lbi[:], in1=lre[:], op=ALU.mult))
            P.op("dve", ["t1", "lamim"], ["t2"], lambda e: e.tensor_tensor(out=t2[:], in0=t1[:], in1=lim[:], op=ALU.mult))
            P.op("dve", ["qi", "t2"], ["qi"], lambda e: e.tensor_tensor(out=qi[:], in0=qi[:], in1=t2[:], op=ALU.subtract))
            P.op("dve", ["qi", "t3"], ["qi"], lambda e: e.tensor_tensor(out=qi[:], in0=qi[:], in1=t3[:], op=ALU.mult))
            th = s5t("th", [128, 64]); lrho = s5t("lrho", [128, 64]); nl8 = s5t("nl8", [128, 64]); rho = s5t("rho", [128, 64])
            kir = s5t("kir", [128, 64]); kii = s5t("kii", [128, 64])
            P.op("dve", ["ai"], ["th"], lambda e: e.tensor_scalar(out=th[:], in0=ai[:], scalar1=8.0, scalar2=None, op0=ALU.mult))
            P.op("dve", ["ar"], ["lrho"], lambda e: e.tensor_scalar(out=lrho[:], in0=ar[:], scalar1=8.0, scalar2=None, op0=ALU.mult))
            P.op("dve", ["ar"], ["nl8"], lambda e: e.tensor_scalar(out=nl8[:], in0=ar[:], scalar1=-8.0, scalar2=None, op0=ALU.mult))
            P.op("act", ["lrho"], ["rho"], lambda e: e.activation(out=rho[:], in_=lrho[:], func=AF.Exp))
            cexp(th[:], ["th"], nl8[:], ["nl8"], kir[:], kii[:], ["kir", "kii"], 64)
            P.op("dve", ["kii"], ["kii"], lambda e: e.tensor_scalar(out=kii[:], in0=kii[:], scalar1=-1.0, scalar2=None, op0=ALU.mult))

            PA = s5t("PA", [128, 8, 8]); PM = s5t("PM", [128, 8, 8])
            pwr = [s5t("pwr%d" % i, [128, 8, 8]) for i in range(2)]
            pwi = [s5t("pwi%d" % i, [128, 8, 8]) for i in range(2)]
            bre = s5t("bre", [128, 8, 16]); bim = s5t("bim", [128, 8, 16])
            cre = s5t("cre", [128, 8, 16]); cim = s5t("cim", [128, 8, 16])
            bbr = s5t("bbr", [128, 8, 16]); bbi = s5t("bbi", [128, 8, 16]); tb1 = s5t("tb1", [128, 8, 16])
            Gr = [s5t("Gr%d" % i, [128, 8, 128]) for i in range(2)]
            Gi = [s5t("Gi%d" % i, [128, 8, 128]) for i in range(2)]
            GSr = s5t("GSr", [128, 8, 128]); GSi = s5t("GSi", [128, 8, 128])
            GSm = s5t("GSm", [128, 2, 2, 128])
            rowm = s5t("rowm", [128, 2])
            P.op("sp", [], ["rowm"], lambda e: e.dma_start(out=rowm[:], in_=T["rowm"][:, :]), dma=True)
            tg1 = s5t("tg1", [128, 8, 128]); tg2 = s5t("tg2", [128, 8, 128])
            GinT = s5t("GinT", [128, 8, 2, 128], BF16)
            GoutB = s5t("GoutB", [128, 8, 2, 128], BF16)
            T0 = s5t("T0", [128, 8, 128], BF16)
            uaP = s5t("uaP", [128, 8, 512], BF16)
            U = s5t("U", [128, 8, 512], BF16)
            Yf = s5t("Yf", [128, 2, 512], F32)
            yP = s5t("yP", [128, 8, 512], F32)
            ang = s5t("ang", [128, 512]); cosT = s5t("cosT", [128, 512]); sinT = s5t("sinT", [128, 512])
            m1 = s5t("m1", [128, 512]); m2 = s5t("m2", [128, 512]); Vr = s5t("Vr", [128, 512]); Vi = s5t("Vi", [128, 512])
            Sr = s5t("Sr", [128, 512]); Si = s5t("Si", [128, 512])
            Zr = s5t("Zr", [128, 516], BF16); Zi = s5t("Zi", [128, 516], BF16)
            P.op("dve", [], ["Zr"], lambda e: e.memset(Zr[:], 0.0))
            P.op("dve", [], ["Zi"], lambda e: e.memset(Zi[:], 0.0))

            def bc3(ap2, n):
                return ap2.unsqueeze(2).to_broadcast([128, 8, n])

            KS5 = int(os.environ.get("KS5", "8"))
            KS5S = os.environ.get("KS5STOP", "Z")
            if KS5S <= "A":
                P.dead = True
            for ft in range(min(8, KS5)):
                gs = slice(ft * 8, (ft + 1) * 8)
                P.op("sp", [("uas", ft)], ["uaP"], lambda e, ft=ft: e.dma_start(out=uaP[:], in_=uas[:, ft, :, :]), dma=True)
                for g in range(8):
                    for j in range(8):
                        P.op("sp", [("uas", ft)], [("U", g)],
                             lambda e, g=g, j=j, ft=ft: e.dma_start(out=U[j * 16:(j + 1) * 16, g, :], in_=uas[g * 16:(g + 1) * 16, ft, j, :]), dma=True)
                for nm, tl in (("s5_bre", bre), ("s5_bim", bim), ("s5_cre", cre), ("s5_cim", cim)):
                    P.op("sp", [], [nm],
                         lambda e, nm=nm, tl=tl: e.dma_start(out=tl[:], in_=T[nm][:, gs, :]), dma=True)
                kb = ["s5_bre", "s5_bim", "s5_cre", "s5_cim"]
                P.op("dve", [kb[0], "qr"], ["bbr"], lambda e: e.tensor_tensor(out=bbr[:], in0=bre[:], in1=bc3(qr[:, gs], 16), op=ALU.mult))
                P.op("dve", [kb[1], "qi"], ["tb1"], lambda e: e.tensor_tensor(out=tb1[:], in0=bim[:], in1=bc3(qi[:, gs], 16), op=ALU.mult))
                P.op("dve", ["bbr", "tb1"], ["bbr"], lambda e: e.tensor_tensor(out=bbr[:], in0=bbr[:], in1=tb1[:], op=ALU.subtract))
                P.op("dve", [kb[1], "qr"], ["bbi"], lambda e: e.tensor_tensor(out=bbi[:], in0=bim[:], in1=bc3(qr[:, gs], 16), op=ALU.mult))
                P.op("dve", [kb[0], "qi"], ["tb1"], lambda e: e.tensor_tensor(out=tb1[:], in0=bre[:], in1=bc3(qi[:, gs], 16), op=ALU.mult))
                P.op("dve", ["bbi", "tb1"], ["bbi"], lambda e: e.tensor_tensor(out=bbi[:], in0=bbi[:], in1=tb1[:], op=ALU.add))
                for io, tau in ((0, tauin), (1, tauout)):
                    P.op("dve", ["ai", "tau%d" % io], ["PA"],
                         lambda e, tau=tau: e.tensor_tensor(out=PA[:], in0=bc3(ai[:, gs], 8), in1=tau[:].unsqueeze(1).to_broadcast([128, 8, 8]), op=ALU.mult))
                    P.op("dve", ["ar", "tau%d" % io], ["PM"],
                         lambda e, tau=tau: e.tensor_tensor(out=PM[:], in0=bc3(ar[:, gs], 8), in1=tau[:].unsqueeze(1).to_broadcast([128, 8, 8]), op=ALU.mult))
                    cexp(PA[:], ["PA"], PM[:], ["PM"], pwr[io][:], pwi[io][:], [("pwr", io), ("pwi", io)], 64,
                         shape=("p (g t) -> p g t", dict(g=8)))
                for io, (cr_, ci_, kr, ki_) in enumerate(((bbr, bbi, "bbr", "bbi"), (cre, cim, kb[2], kb[3]))):
                    def v4(t):
                        return t[:].rearrange("p g (t h) -> p g t h", t=8)
                    pr4 = lambda io=io: pwr[io][:].unsqueeze(3).to_broadcast([128, 8, 8, 16])
                    pi4 = lambda io=io: pwi[io][:].unsqueeze(3).to_broadcast([128, 8, 8, 16])
                    c_r = lambda t=cr_: t[:].unsqueeze(2).to_broadcast([128, 8, 8, 16])
                    c_i = lambda t=ci_: t[:].unsqueeze(2).to_broadcast([128, 8, 8, 16])
                    P.op("dve", [("pwr", io), kr], [("Gr", io)], lambda e, io=io, pr4=pr4, c_r=c_r: e.tensor_tensor(out=v4(Gr[io]), in0=pr4(), in1=c_r(), op=ALU.mult))
                    P.op("pool", [("pwi", io), ki_], ["tg1"], lambda e, pi4=pi4, c_i=c_i: e.tensor_tensor(out=v4(tg1), in0=pi4(), in1=c_i(), op=ALU.mult))
                    P.op("dve", [("Gr", io), "tg1"], [("Gr", io)], lambda e, io=io: e.tensor_tensor(out=Gr[io][:], in0=Gr[io][:], in1=tg1[:], op=ALU.subtract))
                    P.op("dve", [("pwr", io), ki_], [("Gi", io)], lambda e, io=io, pr4=pr4, c_i=c_i: e.tensor_tensor(out=v4(Gi[io]), in0=pr4(), in1=c_i(), op=ALU.mult))
                    P.op("pool", [("pwi", io), kr], ["tg2"], lambda e, pi4=pi4, c_r=c_r: e.tensor_tensor(out=v4(tg2), in0=pi4(), in1=c_r(), op=ALU.mult))
                    if io == 0:
                        P.op("dve", [("Gi", 0), "tg2"], [("Gi", 0)], lambda e: e.tensor_tensor(out=Gi[0][:], in0=Gi[0][:], in1=tg2[:], op=ALU.add))
                    else:
                        P.op("dve", [("Gi", 1), "tg2"], [("Gi", 1)],
                             lambda e: e.scalar_tensor_tensor(out=Gi[1][:], in0=Gi[1][:], scalar=-1.0, in1=tg2[:], op0=ALU.mult, op1=ALU.subtract))
                P.op("dve", [("Gr", 0), "kir"], ["GSr"], lambda e: e.tensor_tensor(out=GSr[:], in0=Gr[0][:], in1=bc3(kir[:, gs], 128), op=ALU.mult))
                P.op("pool", [("Gi", 0), "kii"], ["tg1"], lambda e: e.tensor_tensor(out=tg1[:], in0=Gi[0][:], in1=bc3(kii[:, gs], 128), op=ALU.mult))
                P.op("dve", ["GSr", "tg1"], ["GSr"], lambda e: e.tensor_tensor(out=GSr[:], in0=GSr[:], in1=tg1[:], op=ALU.subtract))
                P.op("dve", [("Gi", 0), "kir"], ["GSi"], lambda e: e.tensor_tensor(out=GSi[:], in0=Gi[0][:], in1=bc3(kir[:, gs], 128), op=ALU.mult))
                P.op("pool", [("Gr", 0), "kii"], ["tg2"], lambda e: e.tensor_tensor(out=tg2[:], in0=Gr[0][:], in1=bc3(kii[:, gs], 128), op=ALU.mult))
                P.op("dve", ["GSi", "tg2"], ["GSi"], lambda e: e.tensor_tensor(out=GSi[:], in0=GSi[:], in1=tg2[:], op=ALU.add))
                P.op("act", [("Gr", 1)], [("GoutB", 0)], lambda e: e.copy(out=GoutB[:, :, 0, :], in_=Gr[1][:]))
                P.op("act", [("Gi", 1)], [("GoutB", 1)], lambda e: e.copy(out=GoutB[:, :, 1, :], in_=Gi[1][:]))
                if KS5S <= "B":
                    P.dead = True
                for g in range(min(8, KS5)):
                    gg = ft * 8 + g
                    pq, pkq = next_pb()
                    if os.environ.get("KNO_TR"):
                        P.mute = True
                    P.op("pe", [("Gr", 0), "identf"], [pkq], lambda e, pq=pq, g=g: e.transpose(out=pq[:, 0:128], in_=Gr[0][:, g, :], identity=identf[:]))
                    P.op("pe", [("Gi", 0), "identf"], [pkq], lambda e, pq=pq, g=g: e.transpose(out=pq[:, 128:256], in_=Gi[0][:, g, :], identity=identf[:]))
                    P.op("act", [pkq], [("GinT", g)],
                         lambda e, pq=pq, g=g: e.copy(out=GinT[:, g, :, :], in_=pq[:, 0:256].rearrange("p (r n) -> p r n", r=2)))
                    P.mute = False
                    if os.environ.get("KNO_T0"):
                        P.mute = True
                    pf, pkf = next_pb()
                    for hh in range(2):
                        P.op("act", ["GSr", "rowm"], [("GSm", hh, 0)], lambda e, hh=hh, g=g: e.activation(out=GSm[:, hh, 0, :], in_=GSr[:, g, :], func=AF.Copy, scale=rowm[:, hh:hh + 1]))
                        P.op("act", ["GSi", "rowm"], [("GSm", hh, 1)], lambda e, hh=hh, g=g: e.activation(out=GSm[:, hh, 1, :], in_=GSi[:, g, :], func=AF.Copy, scale=rowm[:, hh:hh + 1]))
                    for hh in range(2):
                        mm_acc(pf[:, hh * 128:(hh + 1) * 128], pkf,
                               [(GSm[:, hh, 0, :], Gr[1][:, g, :], [("GSm", hh, 0), ("Gr", 1)]),
                                (GSm[:, hh, 1, :], Gi[1][:, g, :], [("GSm", hh, 1), ("Gi", 1)])])
                    P.op("dve", [pkf, "maskf"], ["m1"], lambda e, pf=pf: e.tensor_tensor(out=m1[:, 0:128], in0=pf[:, 0:128], in1=maskf[:], op=ALU.mult))
                    P.op("dve", [pkf, "maskb"], ["m2"], lambda e, pf=pf: e.tensor_tensor(out=m2[:, 0:128], in0=pf[:, 128:256], in1=maskb[:], op=ALU.mult))
                    P.op("dve", ["m1", "m2"], [("T0", g)], lambda e, g=g: e.tensor_tensor(out=T0[:, g, :], in0=m1[:, 0:128], in1=m2[:, 0:128], op=ALU.add))
                    P.mute = False
                    if KS5S <= "C":
                        P.dead = True
                    pxr, pkxr = next_pb()
                    pxi, pkxi = next_pb()
                    mm_acc(pxr[:], pkxr, [(GinT[:, g, 0, :], U[:, g, :], [("GinT", g), ("U", g)])])
                    mm_acc(pxi[:], pkxi, [(GinT[:, g, 1, :], U[:, g, :], [("GinT", g), ("U", g)])])
                    P.op("dve", ["kidx", "th"], ["ang"],
                         lambda e, gg=gg: e.tensor_scalar(out=ang[:], in0=kidx[:], scalar1=th[:, gg:gg + 1], scalar2=None, op0=ALU.mult))
                    cexp(ang[:], ["ang"], None, [], cosT[:], sinT[:], ["cosT", "sinT"], 512)
                    P.op("dve", [pkxr, "cosT"], ["m1"], lambda e, pxr=pxr: e.tensor_tensor(out=m1[:], in0=pxr[:], in1=cosT[:], op=ALU.mult))
                    P.op("dve", [pkxi, "sinT"], ["m2"], lambda e, pxi=pxi: e.tensor_tensor(out=m2[:], in0=pxi[:], in1=sinT[:], op=ALU.mult))
                    P.op("pool", ["m1", "m2"], ["Vr"], lambda e: e.tensor_tensor(out=Vr[:], in0=m1[:], in1=m2[:], op=ALU.add))
                    P.op("dve", [pkxi, "cosT"], ["m1"], lambda e, pxi=pxi: e.tensor_tensor(out=m1[:], in0=pxi[:], in1=cosT[:], op=ALU.mult))
                    P.op("dve", [pkxr, "sinT"], ["m2"], lambda e, pxr=pxr: e.tensor_tensor(out=m2[:], in0=pxr[:], in1=sinT[:], op=ALU.mult))
                    P.op("pool", ["m1", "m2"], ["Vi"], lambda e: e.tensor_tensor(out=Vi[:], in0=m1[:], in1=m2[:], op=ALU.subtract))
                    if KS5S <= "D":
                        P.dead = True
                    for (Vt, St, vk, sk) in ((Vr, Sr, "Vr", "Sr"), (Vi, Si, "Vi", "Si")):
                        P.op("dve", [vk, "rho"], [(sk, 0)],
                             lambda e, Vt=Vt, St=St, gg=gg: e.tensor_tensor_scan(out=St[0:64, :], data0=rho[0:64, gg:gg + 1].to_broadcast([64, 512]),
                                                                                 data1=Vt[0:64, :], initial=0.0, op0=ALU.mult, op1=ALU.add))
                        P.op("dve", [vk, "rho"], [(sk, 1)],
                             lambda e, Vt=Vt, St=St, gg=gg: e.tensor_tensor_scan(out=St[64:128, ::-1], data0=rho[64:128, gg:gg + 1].to_broadcast([64, 512]),
                                                                                 data1=Vt[64:128, ::-1], initial=0.0, op0=ALU.mult, op1=ALU.add))
                    if KS5S <= "E":
                        P.dead = True
                    P.op("dve", ["Sr", "cosT"], ["m1"], lambda e: e.tensor_tensor(out=m1[:], in0=Sr[:], in1=cosT[:], op=ALU.mult))
                    P.op("pool", ["Si", "sinT"], ["m2"], lambda e: e.tensor_tensor(out=m2[:], in0=Si[:], in1=sinT[:], op=ALU.mult))
                    P.op("dve", ["m1", "m2"], [("Zr", 0)], lambda e: e.tensor_tensor(out=Zr[0:64, 2:514], in0=m1[0:64, :], in1=m2[0:64, :], op=ALU.subtract))
                    P.op("dve", ["m1", "m2"], [("Zr", 1)], lambda e: e.tensor_tensor(out=Zr[64:128, 0:512], in0=m1[64:128, :], in1=m2[64:128, :], op=ALU.subtract))
                    P.op("dve", ["Sr", "sinT"], ["m1"], lambda e: e.tensor_tensor(out=m1[:], in0=Sr[:], in1=sinT[:], op=ALU.mult))
                    P.op("pool", ["Si", "cosT"], ["m2"], lambda e: e.tensor_tensor(out=m2[:], in0=Si[:], in1=cosT[:], op=ALU.mult))
                    P.op("dve", ["m1", "m2"], [("Zi", 0)], lambda e: e.tensor_tensor(out=Zi[0:64, 2:514], in0=m1[0:64, :], in1=m2[0:64, :], op=ALU.add))
                    P.op("dve", ["m1", "m2"], [("Zi", 1)], lambda e: e.tensor_tensor(out=Zi[64:128, 0:512], in0=m1[64:128, :], in1=m2[64:128, :], op=ALU.add))
                    py, pky = next_pb()
                    mm_acc(py[:], pky, [(T0[:, g, :], U[:, g, :], [("T0", g), ("U", g)]),
                                        (GoutB[:, g, 0, :], Zr[:, 1:513], [("GoutB", 0), "Zr"]),
                                        (GoutB[:, g, 1, :], Zi[:, 1:513], [("GoutB", 1), "Zi"])])
                    P.op("act", [pky], [("Yf", g % 2)], lambda e, py=py, g=g: e.copy(out=Yf[:, g % 2, :], in_=py[:]))
                    for i in range(8):
                        P.op("sp", [("Yf", g % 2)], [("yP", g)],
                             lambda e, g=g, i=i: e.dma_start(out=yP[g * 16:(g + 1) * 16, i, :], in_=Yf[i * 16:(i + 1) * 16, g % 2, :]), dma=True)
                if KS5S <= "F":
                    P.dead = True
                P.op("dve", ["yP", "uaP", "dcol"], ["yP"],
                     lambda e, ft=ft: e.scalar_tensor_tensor(out=yP[:], in0=uaP[:], scalar=dcol[:, ft:ft + 1], in1=yP[:], op0=ALU.mult, op1=ALU.add))
                P.op("act", ["yP"], [("gaT", ft)],
                     lambda e, ft=ft: e.activation(out=gaT[:, ft, :].rearrange("p (c j) -> p j c", j=8), in_=yP[:], func=AF.Gelu_apprx_tanh))

    P.barrier()
    if "dbg_ga" in T:
        for ft in range(8):
            P.op("sp", [("gaT", ft)], [("dbg", ft)], lambda e, ft=ft: e.dma_start(out=T["dbg_ga"][:, ft, :], in_=gaT[:, ft, :]), dma=True)
    if KSTOP <= 4:
        P.dead = True
    col2 = sbt(c34, "col2", [128, 8], F32)
    P.op("sp", [], ["col2"], lambda e: e.dma_start(out=col2[:], in_=T["glub"][:, :]), dma=True)
    for sub in range(2):
        P.barrier()
        with ExitStack() as c4:
            w1 = sbt(c4, "w1_%d" % sub, [128, 8, D], BF16)
            w2 = sbt(c4, "w2_%d" % sub, [128, 8, D], BF16)
            w3 = sbt(c4, "w3_%d" % sub, [128, 8, D], BF16)
            if sub == 0:
                load_cast(w1, ("w1",), T["s5_glu_w"], 8, D)
                load_cast(w2, ("w2",), T["w_a_out"], 8, D)
                load_cast(w3, ("w3",), T["w_in"][:, 3 * D:4 * D], 8, D)
            else:
                load_cast(w1, ("w1",), T["w_b_out"], 8, D)
                load_cast(w2, ("w2",), T["w_o"], 8, D)
                load_cast(w3, ("w3",), T["w_in"][:, 4 * D:5 * D], 8, D)
            unt = sbt(c4, "unt%d" % sub, [128, 8, 512], BF16)
            xt4 = sbt(c4, "xt4_%d" % sub, [128, 4, D], F32)
            a2 = sbt(c4, "a2_%d" % sub, [128, 8, 512], BF16)
            sg = sbt(c4, "sg_%d" % sub, [128, 512], F32)
            sg2 = sbt(c4, "sg2_%d" % sub, [128, 512], F32)
            for tt in range(8):
                sl = slice(tt * 512, (tt + 1) * 512)
                for b4 in range(4):
                    tb = tt * 4 + b4
                    P.op("sp", [], [("xt4", b4)],
                         lambda e, b4=b4, tb=tb: e.dma_start(out=xt4[:, b4, :], in_=xd[tb * 128:(tb + 1) * 128, :]), dma=True)
                    norm_T(xt4[:, b4, :], ("xt4", b4), 0, unt[:, :, b4 * 128:(b4 + 1) * 128], ("unt", b4), ntmp)
                if sub == 0:
                    for m in range(8):
                        pz, pkz = next_pb()
                        mm_acc(pz[:], pkz, [(w1[:, k, m * 128:(m + 1) * 128], gaT[:, k, sl], [("w1", k), ("gaT", k)]) for k in range(8)])
                        P.op("act", [pkz, "col2"], ["sg"], lambda e, pz=pz, m=m: e.activation(out=sg[:], in_=pz[:], func=AF.Sigmoid, bias=col2[:, m:m + 1]))
                        P.op("dve", ["sg", ("gaT", m)], [("a2", m)], lambda e, m=m: e.tensor_tensor(out=a2[:, m, :], in0=gaT[:, m, sl], in1=sg[:], op=ALU.mult))
                    for m in range(8):
                        pm_, pkm = next_pb()
                        mm_acc(pm_[:], pkm, [(w3[:, k, m * 128:(m + 1) * 128], unt[:, k, :], [("w3", k), "unt"]) for k in range(8)])
                        P.op("act", [pkm], ["sg"], lambda e, pm_=pm_: e.activation(out=sg[:], in_=pm_[:], func=AF.Sigmoid))
                        pa_, pka = next_pb()
                        mm_acc(pa_[:], pka, [(w2[:, k, m * 128:(m + 1) * 128], a2[:, k, :], [("w2", k), ("a2", k)]) for k in range(8)])
                        P.op("dve", [pka, "sg"], [("gaT", m)], lambda e, pa_=pa_, m=m: e.tensor_tensor(out=gaT[:, m, sl], in0=pa_[:], in1=sg[:], op=ALU.mult))
                else:
                    P.op("sp", ["ybs"], ["a2"], lambda e: e.dma_start(out=a2[:], in_=ybs[:, :, sl]), dma=True)
                    mg = unt
                    for m in range(8):
                        pm_, pkm = next_pb()
                        mm_acc(pm_[:], pkm, [(w3[:, k, m * 128:(m + 1) * 128], unt[:, k, :], [("w3", k), "unt"]) for k in range(8)])
                        P.op("act", [pkm], ["sg"], lambda e, pm_=pm_: e.activation(out=sg[:], in_=pm_[:], func=AF.Sigmoid))
                        pa_, pka = next_pb()
                        mm_acc(pa_[:], pka, [(w1[:, k, m * 128:(m + 1) * 128], a2[:, k, :], [("w1", k), "a2"]) for k in range(8)])
                        P.op("dve", [pka, "sg"], ["sg2"], lambda e, pa_=pa_: e.tensor_tensor(out=sg2[:], in0=pa_[:], in1=sg[:], op=ALU.mult))
                        P.op("dve", ["sg2", ("gaT", m)], [("gaT", m)], lambda e, m=m: e.tensor_tensor(out=gaT[:, m, sl], in0=gaT[:, m, sl], in1=sg2[:], op=ALU.add))
                    if "dbg_mg" in T:
                        P.op("sp", ["gaT"], [("dbgm", tt)], lambda e, sl=sl: e.dma_start(out=T["dbg_mg"][:, :, sl], in_=gaT[:, :, sl]), dma=True)
                    for b4 in range(4):
                        tb = tt * 4 + b4
                        for n2 in range(2):
                            po, pko = next_pb()
                            mm_acc(po[:], pko, [(gaT[:, k, tt * 512 + b4 * 128: tt * 512 + (b4 + 1) * 128], w2[:, k, n2 * 512:(n2 + 1) * 512],
                                                 [("w2", k), ("gaT", k)]) for k in range(8)])
                            P.op("dve", [pko, ("xt4", b4)], [("xt4", b4)],
                                 lambda e, po=po, b4=b4, n2=n2: e.tensor_tensor(out=xt4[:, b4, n2 * 512:(n2 + 1) * 512], in0=po[:],
                                                                                in1=xt4[:, b4, n2 * 512:(n2 + 1) * 512], op=ALU.add))
                        P.op("sp", [("xt4", b4)], [("yd", tb)],
                             lambda e, b4=b4, tb=tb: e.dma_start(out=yd[tb * 128:(tb + 1) * 128, :], in_=xt4[:, b4, :]), dma=True)

    P.barrier()
    c34.close()
    if KSTOP <= 5:
        P.dead = True
    with ExitStack() as c5:
        wg = sbt(c5, "wg", [128, 8, FH], BF16)
        wu = sbt(c5, "wu", [128, 8, FH], BF16)
        wd = sbt(c5, "wd", [128, 22, D], BF16)
        load_cast(wg, ("wg",), T["w_ff_gate"], 8, FH)
        load_cast(wu, ("wu",), T["w_ff_up"], 8, FH)
        load_cast(wd, ("wd",), T["w_ff_down"], 22, D)
        vT = sbt(c5, "vT", [128, 8, 512], BF16)
        h4 = sbt(c5, "h4", [128, 4, D], F32)
        hT = sbt(c5, "hT", [128, 22, 512], BF16)
        sgf = sbt(c5, "sgf", [128, 512], F32)
        for tt in range(8):
            for b4 in range(4):
                tb = tt * 4 + b4
                P.op("sp", [("yd", tb)], [("h4", b4)],
                     lambda e, b4=b4, tb=tb: e.dma_start(out=h4[:, b4, :], in_=yd[tb * 128:(tb + 1) * 128, :]), dma=True)
                norm_T(h4[:, b4, :], ("h4", b4), 1, vT[:, :, b4 * 128:(b4 + 1) * 128], ("vT", b4), ntmp)
            for m in range(22):
                pg_, pkg = next_pb()
                mm_acc(pg_[:], pkg, [(wg[:, k, m * 128:(m + 1) * 128], vT[:, k, :], [("wg", k), "vT"]) for k in range(8)])
                P.op("act", [pkg], ["sgf"], lambda e, pg_=pg_: e.activation(out=sgf[:], in_=pg_[:], func=AF.Silu))
                pu_, pku = next_pb()
                mm_acc(pu_[:], pku, [(wu[:, k, m * 128:(m + 1) * 128], vT[:, k, :], [("wu", k), "vT"]) for k in range(8)])
                P.op("dve", [pku, "sgf"], [("hT", m)], lambda e, pu_=pu_, m=m: e.tensor_tensor(out=hT[:, m, :], in0=pu_[:], in1=sgf[:], op=ALU.mult))
            for b4 in range(4):
                tb = tt * 4 + b4
                for n2 in range(2):
                    po, pko = next_pb()
                    mm_acc(po[:], pko, [(hT[:, m, b4 * 128:(b4 + 1) * 128], wd[:, m, n2 * 512:(n2 + 1) * 512], [("wd", m), ("hT", m)])
                                        for m in range(22)])
                    P.op("dve", [pko, ("h4", b4)], [("h4", b4)],
                         lambda e, po=po, b4=b4, n2=n2: e.tensor_tensor(out=h4[:, b4, n2 * 512:(n2 + 1) * 512], in0=po[:],
                                                                        in1=h4[:, b4, n2 * 512:(n2 + 1) * 512], op=ALU.add))
                P.op("sp", [("h4", b4)], [("yd", tb)],
                     lambda e, b4=b4, tb=tb: e.dma_start(out=yd[tb * 128:(tb + 1) * 128, :], in_=h4[:, b4, :]), dma=True)

    outs = []
    P.barrier()
    with ExitStack() as c6:
        wpg = sbt(c6, "wpg", [128, 8, D], BF16)
        wpp = sbt(c6, "wpp", [128, 2, D], BF16)
        load_cast(wpg, ("wpg",), T["w_ple_gate"], 8, D)
        load_cast(wpp, ("wpp",), T["w_ple_proj"], 2, D)
        gfin = sbt(c6, "gfin", [128, D], F32)
        P.op("sp", [], ["gfin"], lambda e: e.dma_start(out=gfin[:], in_=T["gfin"][:, :]), dma=True)
        h6 = [sbt(c6, "h6_%d" % i, [128, D], F32) for i in range(2)]
        pt_ = [sbt(c6, "pt_%d" % i, [128, 256], F32) for i in range(2)]
        ptb = sbt(c6, "ptb", [128, 256], BF16)
        pT = sbt(c6, "pTt", [128, 2, 128], BF16)
        wT = sbt(c6, "wTt", [128, 8, 128], BF16)
        gp = sbt(c6, "gp", [128, 512], F32)
        ob = [sbt(c6, "ob%d" % i, [128, D], F32) for i in range(2)]
        for tb in range(32):
            h = h6[tb % 2]
            pp = pt_[tb % 2]
            o_ = ob[tb % 2]
            P.op("sp", [("yd", tb)], [("h6", tb % 2)], lambda e, h=h, tb=tb: e.dma_start(out=h[:], in_=yd[tb * 128:(tb + 1) * 128, :]), dma=True)
            P.op("sp", [], [("pt_", tb % 2)], lambda e, pp=pp, tb=tb: e.dma_start(out=pp[:], in_=T["p"][tb * 128:(tb + 1) * 128, :]), dma=True)
            norm_T(h[:], ("h6", tb % 2), 2, wT[:], ("wT",), ntmp)
            P.op("act", [("pt_", tb % 2)], ["ptb"], lambda e, pp=pp: e.copy(out=ptb[:], in_=pp[:]))
            ptile, pkey = next_pt()
            for k in range(2):
                P.op("pe", ["ptb", "ident"], [pkey],
                     lambda e, k=k, ptile=ptile: e.transpose(out=ptile[:, k * 128:(k + 1) * 128], in_=ptb[:, k * 128:(k + 1) * 128], identity=ident[:]))
            P.op("act", [pkey], ["pTt"], lambda e, ptile=ptile: e.copy(out=pT[:], in_=ptile[:, 0:256].rearrange("p (k t) -> p k t", k=2)))
            for n2 in range(2):
                nsl = slice(n2 * 512, (n2 + 1) * 512)
                pg_, pkg = next_pb()
                mm_acc(pg_[:], pkg, [(wT[:, k, :], wpg[:, k, nsl], [("wpg", k), "wT"]) for k in range(8)])
                P.op("act", [pkg], ["gp"], lambda e, pg_=pg_: e.activation(out=gp[:], in_=pg_[:], func=AF.Sigmoid))
                pq_, pkq = next_pb()
                mm_acc(pq_[:], pkq, [(pT[:, k, :], wpp[:, k, nsl], [("wpp", k), "pTt"]) for k in range(2)])
                P.op("dve", [pkq, "gp"], ["gp"], lambda e, pq_=pq_: e.tensor_tensor(out=gp[:], in0=pq_[:], in1=gp[:], op=ALU.mult))
                P.op("dve", ["gp", ("h6", tb % 2)], [("h6", tb % 2)],
                     lambda e, h=h, nsl=nsl: e.tensor_tensor(out=h[:, nsl], in0=h[:, nsl], in1=gp[:], op=ALU.add))
            junk, ss, rs, rr, xn = ntmp
            P.op("act", [("h6", tb % 2)], ["junk"], lambda e, h=h: e.activation(out=junk[:], in_=h[:], func=AF.Square))
            P.op("dve", ["junk"], ["ss"], lambda e: e.tensor_reduce(out=ss[:], in_=junk[:], axis=AX.X, op=ALU.add))
            P.op("act", ["ss", "eps"], ["rs"], lambda e: e.activation(out=rs[:], in_=ss[:], func=AF.Sqrt, scale=1.0 / D, bias=eps[:]))
            P.op("dve", ["rs"], ["rr"], lambda e: e.reciprocal(out=rr[:], in_=rs[:]))
            P.op("dve", [("h6", tb % 2), "rr", "gfin"], [("ob", tb % 2)],
                 lambda e, h=h, o_=o_: e.scalar_tensor_tensor(out=o_[:], in0=h[:], scalar=rr[:, 0:1], in1=gfin[:], op0=ALU.mult, op1=ALU.mult))
            outs.append(P.op("sp", [("ob", tb % 2)], [("yd", tb)],
                             lambda e, o_=o_, tb=tb: e.dma_start(out=yd[tb * 128:(tb + 1) * 128, :], in_=o_[:]), dma=True))
    P.final_wait("sp", outs)


DRAM_SPECS = None


def make_nc(shapes):
    info = None
    for plan in (True, False):
        nc = bass.Bass("TRN2", target_bir_lowering=False)
        T = {}
        for name, (shape, dt) in shapes.items():
            T[name] = nc.dram_tensor(name, list(shape), dt, kind="ExternalInput").ap()
        T["y"] = nc.dram_tensor("y", [S, D], F32, kind="ExternalOutput").ap()
        T["yb_scr"] = nc.dram_tensor("yb_scr", [128, 8, S], BF16, kind="Internal").ap()
        T["ua_scr"] = nc.dram_tensor("ua_scr", [128, 8, 8, 512], BF16, kind="Internal").ap()
        with ExitStack() as ctx, ExitStack() as semctx:
            nc._ctx = ctx
            nc._semctx = semctx
            P = Prog(nc, plan, info)
            try:
                build(nc, P, T)
            except _Stop:
                if getattr(nc, "_late", None) is not None:
                    nc._late.close()
            info = P.signal
    return nc


def host_layout(inp):
    f = lambda a: np.ascontiguousarray(np.asarray(a, dtype=np.float32))
    sh = {}
    sh["w_in"] = f(inp["w_in"][0])
    for k in ("s5_glu_w", "w_a_out", "w_b_out", "w_o", "w_ff_gate", "w_ff_up", "w_ff_down", "w_ple_gate", "w_ple_proj"):
        sh[k] = f(inp[k][0])
    sh["lru_wa"] = f(inp["lru_wa"][0])
    sh["lru_wx"] = f(inp["lru_wx"][0])
    col = lambda v: np.asarray(v, np.float32).reshape(8, 128).T
    sh["gcols"] = f(np.stack([col(inp["g_mix"][0]), col(inp["g_ffn"][0]), col(inp["g_ple"][0])], axis=1))
    sh["glub"] = f(col(inp["s5_glu_b"][0]))
    sh["s5_dcol"] = f(col(inp["s5_d"][0]))
    sh["gfin"] = f(np.broadcast_to(np.asarray(inp["g_final"], np.float32)[None, :], (128, D)))
    ls = np.zeros((128, 8, 16), np.float32)
    for k in range(4):
        ls[:, :, k] = col(inp["lru_conv_w"][0, k])
    ls[:, :, 4] = col(inp["lru_conv_b"][0])
    for d in range(2):
        ls[:, :, 5 + d] = col(inp["lru_ba"][0, d])
        ls[:, :, 7 + d] = col(inp["lru_bx"][0, d])
        ls[:, :, 9 + d] = col(inp["lru_a_logit"][0, d])
    sh["lru_small"] = ls
    t2 = lambda a: f(np.asarray(a, np.float32).transpose(0, 2, 1).reshape(128, 64))
    sh["s5_lamre"] = t2(inp["s5_lambda_re"][0])
    sh["s5_lamim"] = t2(inp["s5_lambda_im"][0])
    sh["s5_logdt"] = f(np.broadcast_to(np.asarray(inp["s5_log_dt"][0], np.float32)[:, None, :], (2, 64, 64)).reshape(128, 64))
    sh["s5_bre"] = f(np.asarray(inp["s5_b_re"][0], np.float32).transpose(0, 2, 1, 3).reshape(128, 64, 16))
    sh["s5_bim"] = f(np.asarray(inp["s5_b_im"][0], np.float32).transpose(0, 2, 1, 3).reshape(128, 64, 16))
    sh["s5_cre"] = f(np.asarray(inp["s5_c_re"][0], np.float32).transpose(0, 3, 1, 2).reshape(128, 64, 16))
    sh["s5_cim"] = f(np.asarray(inp["s5_c_im"][0], np.float32).transpose(0, 3, 1, 2).reshape(128, 64, 16))
    sh["identf"] = np.eye(128, dtype=np.float32)
    jj = np.arange(128) // 16
    sh["maskf"] = (jj[None, :] >= jj[:, None]).astype(np.float32)
    sh["maskb"] = (jj[None, :] <= jj[:, None]).astype(np.float32)
    kidx = np.zeros((128, 512), np.float32)
    kidx[:64] = np.arange(512, dtype=np.float32)[None, :]
    kidx[64:] = (511 - np.arange(512, dtype=np.float32))[None, :]
    sh["kidx"] = kidx
    tin = np.zeros((128, 8), np.float32); tout = np.zeros((128, 8), np.float32)
    tin[:64] = 7 - np.arange(8); tin[64:] = np.arange(8)
    tout[:64] = np.arange(8) + 1; tout[64:] = 8 - np.arange(8)
    rowm = np.zeros((128, 2), np.float32); rowm[:64, 0] = 1.0; rowm[64:, 1] = 1.0
    sh["rowm"] = rowm
    sh["tauin"] = tin
    sh["tauout"] = tout
    return sh


_NC_CACHE = {}


def kernel(**inputs):
    shared = host_layout(inputs)
    x = np.asarray(inputs["x"], np.float32)
    p = np.asarray(inputs["p"], np.float32)[0]
    shapes = {k: (v.shape, F32) for k, v in shared.items()}
    shapes["x"] = ((S, D), F32)
    shapes["p"] = ((S, 256), F32)
    if "nc" not in _NC_CACHE:
        _NC_CACHE["nc"] = make_nc(shapes)
    nc = _NC_CACHE["nc"]
    in_maps = []
    for c in range(NCORE):
        m = dict(shared)
        m["x"] = np.ascontiguousarray(x[c])
        m["p"] = np.ascontiguousarray(p[c])
        in_maps.append(m)
    res = run_bass_kernel_spmd(nc, in_maps, core_ids=list(range(NCORE)))
    return np.stack([np.asarray(r["y"], np.float32) for r in res.results], axis=0)
```

```python
import math
import numpy as np
from contextlib import ExitStack
import concourse.bass as bass
import concourse.mybir as mybir
from concourse.bass_utils import run_bass_kernel_spmd

F32 = mybir.dt.float32
BF16 = mybir.dt.bfloat16
I32 = mybir.dt.int32
AF = mybir.ActivationFunctionType
ALU = mybir.AluOpType
AX = mybir.AxisListType

S = 4096
D = 1024
NCORE = 8
FH = 2816
ENGS = ("pe", "act", "dve", "pool", "sp")
NLANES = 24
TWO_PI = 2.0 * math.pi


class Prog:
    def __init__(self, nc, plan, info=None):
        self.nc = nc
        self.plan = plan
        self.idx = 0
        self.res = {}
        self.clock = {}
        self.eclock = {e: {} for e in ENGS}
        self.stream_of = {}
        self.last = {}
        self.lane_rr = 0
        if plan:
            self.signal = set()
        else:
            self.signal = info
            self.cnt = {}
            self.sigval = {}
            self.sems = {}
            self.engh = {"pe": nc.tensor, "act": nc.scalar, "dve": nc.vector,
                         "pool": nc.gpsimd, "sp": nc.sync}

    def sem(self, stream):
        if stream not in self.sems:
            self.sems[stream] = self.nc._semctx.enter_context(self.nc.semaphore("s_" + stream))
        return self.sems[stream]

    def _conf(self, key):
        d = self.res.setdefault(key[0], {})
        out = []
        for k, v in d.items():
            n = min(len(k), len(key))
            if k[:n] == key[:n]:
                out.append((k, v))
        return d, out

    def op(self, eng, reads, writes, emit, dma=False):
        if getattr(self, "mute", False) or getattr(self, "dead", False):
            return -1
        i = self.idx
        self.idx += 1
        if dma:
            lane = self.lane_rr % NLANES
            self.lane_rr += 1
            stream = "lane%d" % lane
            writes = list(writes) + [("__lane", lane)]
        else:
            stream = eng
        self.stream_of[i] = stream
        deps = set()
        for key in reads:
            key = key if isinstance(key, tuple) else (key,)
            d, confs = self._conf(key)
            for k, v in confs:
                if v[0] is not None:
                    deps.add(v[0])
            d.setdefault(key, [None, []])[1].append(i)
        for key in writes:
            key = key if isinstance(key, tuple) else (key,)
            d, confs = self._conf(key)
            for k, v in confs:
                if v[0] is not None:
                    deps.add(v[0])
                deps.update(v[1])
                del d[k]
            d[key] = [i, []]
        deps.discard(i)
        ec = self.eclock[eng]
        need = []
        for y in sorted(deps, reverse=True):
            sy = self.stream_of[y]
            if sy == "pe" and stream == "pe":
                continue
            if ec.get(sy, -1) >= y:
                continue
            need.append(y)
            for s, v in self.clock[y].items():
                if ec.get(s, -1) < v:
                    ec[s] = v
        if self.plan:
            self.signal.update(need)
            if dma:
                self.signal.add(i)
        else:
            h = self.engh[eng]
            for y in need:
                h.wait_ge(self.sem(self.stream_of[y]), self.sigval[y])
            ins = emit(h)
            if i in self.signal:
                inc = 16 if dma else 1
                c = self.cnt.get(stream, 0) + inc
                self.cnt[stream] = c
                self.sigval[i] = c
                ins.then_inc(self.sem(stream), inc)
        ck = dict(ec)
        ck[stream] = i
        self.clock[i] = ck
        self.last[stream] = i
        return i

    def barrier(self):
        if getattr(self, "mute", False) or getattr(self, "dead", False):
            return
        lasts = dict(self.last)
        for eng in ENGS:
            ec = self.eclock[eng]
            for stream, y in lasts.items():
                if stream == eng and eng == "pe":
                    continue
                if ec.get(stream, -1) >= y:
                    continue
                if self.plan:
                    self.signal.add(y)
                else:
                    self.engh[eng].wait_ge(self.sem(stream), self.sigval[y])
                for s_, v in self.clock[y].items():
                    if ec.get(s_, -1) < v:
                        ec[s_] = v

    def final_wait(self, eng, ops):
        ops = [o for o in ops if o >= 0]
        if self.plan:
            self.signal.update(ops)
            return
        h = self.engh[eng]
        for y in ops:
            h.wait_ge(self.sem(self.stream_of[y]), self.sigval[y])


class _Stop(Exception):
    pass


import os
KSTOP = int(os.environ.get("KSTOP", "99"))


def build(nc, P, T):
    top = nc._ctx

    def sbt(ctx, name, shape, dt):
        return ctx.enter_context(nc.sbuf_tensor("sb_" + name, shape, dt))

    pb = [top.enter_context(nc.psum_tensor("pb%d" % i, [128, 512], F32)) for i in range(6)]
    pt = [top.enter_context(nc.psum_tensor("pt%d" % i, [128, 1024], BF16)) for i in range(2)]
    cnt = {"pb": 0, "pt": 0, "stg": 0}

    def next_pb():
        i = cnt["pb"] % 6
        cnt["pb"] += 1
        return pb[i], ("pb", i)

    def next_pt():
        i = cnt["pt"] % 2
        cnt["pt"] += 1
        return pt[i], ("pt", i)

    ident = sbt(top, "ident", [128, 128], BF16)
    identf = sbt(top, "identf", [128, 128], F32)
    eps = sbt(top, "eps", [128, 1], F32)
    ones = sbt(top, "ones", [128, 1], F32)
    halfpi = sbt(top, "halfpi", [128, 1], F32)
    gcols = sbt(top, "gcols", [128, 3, 8], F32)
    stage = [sbt(top, "stage%d" % i, [128, 1024], F32) for i in range(3)]

    P.op("sp", [], ["identf"], lambda e: e.dma_start(out=identf[:], in_=T["identf"][:, :]), dma=True)
    P.op("dve", ["identf"], ["ident"], lambda e: e.tensor_copy(out=ident[:], in_=identf[:]))
    P.op("dve", [], ["eps"], lambda e: e.memset(eps[:], 1e-6))
    P.op("dve", [], ["ones"], lambda e: e.memset(ones[:], 1.0))
    P.op("dve", [], ["halfpi"], lambda e: e.memset(halfpi[:], math.pi / 2))
    P.op("sp", [], ["gcols"], lambda e: e.dma_start(out=gcols[:], in_=T["gcols"][:, :, :]), dma=True)

    def load_cast(dst, dkey, src, K, N):
        for k in range(K):
            for c0 in range(0, N, 1024):
                cw = min(1024, N - c0)
                si = cnt["stg"] % 3
                ce = ("pool", "dve", "act")[cnt["stg"] % 3]
                cnt["stg"] += 1
                st = stage[si]
                P.op("sp", [], [("stage", si)],
                     lambda e, st=st, k=k, c0=c0, cw=cw: e.dma_start(out=st[:, 0:cw], in_=src[k * 128:(k + 1) * 128, c0:c0 + cw]), dma=True)
                if ce == "act":
                    P.op("act", [("stage", si)], [dkey + (k, c0)],
                         lambda e, st=st, k=k, c0=c0, cw=cw: e.copy(out=dst[:, k, c0:c0 + cw], in_=st[:, 0:cw]))
                else:
                    P.op(ce, [("stage", si)], [dkey + (k, c0)],
                         lambda e, st=st, k=k, c0=c0, cw=cw: e.tensor_copy(out=dst[:, k, c0:c0 + cw], in_=st[:, 0:cw]))

    def mm_acc(out_ap, okey, terms):
        n = len(terms)
        for i, (l, r, rk) in enumerate(terms):
            P.op("pe", rk, [okey],
                 lambda e, l=l, r=r, i=i: e.matmul(out=out_ap, lhsT=l, rhs=r, start=(i == 0), stop=(i == n - 1)))

    nctr = {"i": 0}

    def mk_ntmp(ctx, tag, nsets):
        sets = []
        for i in range(nsets):
            sets.append((i, sbt(ctx, "junk%s%d" % (tag, i), [128, D], F32), sbt(ctx, "ss%s%d" % (tag, i), [128, 1], F32),
                         sbt(ctx, "rs%s%d" % (tag, i), [128, 1], F32), sbt(ctx, "rr%s%d" % (tag, i), [128, 1], F32),
                         sbt(ctx, "xn%s%d" % (tag, i), [128, D], BF16)))
        return sets

    def norm_T(src, skey, gi, dst, dkey, tmps):
        si, junk, ss, rs, rr, xn = tmps[nctr["i"] % len(tmps)]
        nctr["i"] += 1
        P.op("act", [skey], [("junk", si)], lambda e: e.activation(out=junk[:], in_=src, func=AF.Square))
        P.op("dve", [("junk", si)], [("ss", si)], lambda e: e.tensor_reduce(out=ss[:], in_=junk[:], axis=AX.X, op=ALU.add))
        P.op("act", [("ss", si), "eps"], [("rs", si)],
             lambda e: e.activation(out=rs[:], in_=ss[:], func=AF.Sqrt, scale=1.0 / D, bias=eps[:]))
        P.op("dve", [("rs", si)], [("rr", si)], lambda e: e.reciprocal(out=rr[:], in_=rs[:]))
        P.op("act", [skey, ("rr", si)], [("xn", si)], lambda e: e.activation(out=xn[:], in_=src, func=AF.Copy, scale=rr[:, 0:1]))
        ptile, pkey = next_pt()
        for k in range(8):
            P.op("pe", [("xn", si), "ident"], [pkey],
                 lambda e, k=k: e.transpose(out=ptile[:, k * 128:(k + 1) * 128], in_=xn[:, k * 128:(k + 1) * 128],
                                            identity=ident[:]))
        if gi is None:
            P.op("act", [pkey], [dkey], lambda e: e.copy(out=dst, in_=ptile[:].rearrange("p (k t) -> p k t", k=8)))
        else:
            P.op("dve", [pkey, "gcols"], [dkey],
                 lambda e: e.tensor_tensor(out=dst, in0=ptile[:].rearrange("p (k t) -> p k t", k=8),
                                           in1=gcols[:, gi, :].unsqueeze(2).to_broadcast([128, 8, 128]), op=ALU.mult))
        return rr


    xd = T["x"]
    yd = T["y"]
    ybs = T["yb_scr"]

    uas = T["ua_scr"]
    KSKIP = int(os.environ.get("KSKIP", "0"))
    with ExitStack() as c13:
        P.mute = bool(KSKIP)
        unT = sbt(c13, "unT", [128, 8, S], BF16)
        with ExitStack() as c1:
            xt2 = [sbt(c1, "xt%d" % i, [128, D], F32) for i in range(2)]
            ntmp = mk_ntmp(c1, "p1", 2)
            for tb in range(32):
                xt = xt2[tb % 2]
                P.op("sp", [], [("xt", tb % 2)],
                     lambda e, xt=xt, tb=tb: e.dma_start(out=xt[:], in_=xd[tb * 128:(tb + 1) * 128, :]), dma=True)
                norm_T(xt[:], ("xt", tb % 2), 0, unT[:, :, tb * 128:(tb + 1) * 128], ("unT", tb // 4, tb % 4), ntmp)

        P.barrier()
        if KSTOP <= 1:
            P.dead = True
        with ExitStack() as c3a:
            wua = sbt(c3a, "wua", [128, 8, 128], BF16)
            uaS = sbt(c3a, "uaS", [128, 8, 512], BF16)
            for ft in range(8):
                load_cast(wua, ("wua",), T["w_in"][:, ft * 128:(ft + 1) * 128], 8, 128)
                for tt in range(8):
                    pu, pku = next_pb()
                    mm_acc(pu[:], pku, [(wua[:, k, :], unT[:, k, tt * 512:(tt + 1) * 512], [("wua", k), ("unT", tt)]) for k in range(8)])
                    P.op("act", [pku], [("uaS", tt)],
                         lambda e, pu=pu, tt=tt: e.copy(out=uaS[:, :, tt * 64:(tt + 1) * 64],
                                                        in_=pu[:].rearrange("p (c j) -> p j c", j=8)))
                P.op("sp", ["uaS"], [("uas", ft)], lambda e, ft=ft: e.dma_start(out=uas[:, ft, :, :], in_=uaS[:]), dma=True)

        P.barrier()
        if KSTOP <= 2:
            P.dead = True
        with ExitStack() as c2:
            lsm = sbt(c2, "lsm", [128, 8, 16], F32)
            P.op("sp", [], ["lsm"], lambda e: e.dma_start(out=lsm[:], in_=T["lru_small"][:, :, :]), dma=True)
            cneg = sbt(c2, "cneg", [128, 8, 2], F32)
            cn2 = sbt(c2, "cn2", [128, 8, 2], F32)
            et = sbt(c2, "et", [128, 8, 2], F32)
            P.op("act", ["lsm"], ["et"], lambda e: e.activation(out=et[:], in_=lsm[:, :, 9:11], func=AF.Exp, scale=-1.0))
            P.op("act", ["et", "ones"], ["cn2"], lambda e: e.activation(out=cn2[:], in_=et[:], func=AF.Ln, bias=ones[:]))
            P.op("dve", ["cn2"], ["cneg"], lambda e: e.tensor_scalar(out=cneg[:], in0=cn2[:], scalar1=-8.0, scalar2=None, op0=ALU.mult))
            P.op("dve", ["cneg"], ["cn2"], lambda e: e.tensor_scalar(out=cn2[:], in0=cneg[:], scalar1=2.0, scalar2=None, op0=ALU.mult))

            wxb = sbt(c2, "wxb", [128, 8, 256], BF16)
            wgb = sbt(c2, "wgb", [128, 8, 256], BF16)
            wgt = sbt(c2, "wgt", [128, 2, 2, 2, 256], BF16)
            xb = sbt(c2, "xb", [128, S + 4], F32)
            xc = sbt(c2, "xc", [128, 2, S], F32)
            xcb = sbt(c2, "xcb", [128, 2, S], BF16)
            hb = sbt(c2, "hb", [128, S], F32)
            Q = 1024
            ra = sbt(c2, "ra", [128, Q], F32)
            ri = sbt(c2, "ri", [128, Q], F32)
            rm = sbt(c2, "rm", [128, Q], F32)
            gtm = sbt(c2, "gtm", [128, 512], F32)
            carry = sbt(c2, "carry", [128, 1], F32)
            ybt = sbt(c2, "ybt", [128, S], BF16)
            P.op("dve", [], ["xb"], lambda e: e.memset(xb[:], 0.0))
            for hdx in range(4):
                load_cast(wxb, ("wxb",), T["w_in"][:, D + hdx * 256: D + (hdx + 1) * 256], 8, 256)
                load_cast(wgb, ("wgb",), T["w_in"][:, 2 * D + hdx * 256: 2 * D + (hdx + 1) * 256], 8, 256)
                for d in range(2):
                    for gt, nm in enumerate(("lru_wa", "lru_wx")):
                        for kin in range(2):
                            si = cnt["stg"] % 3
                            cnt["stg"] += 1
                            st = stage[si]
                            P.op("sp", [], [("stage", si)],
                                 lambda e, st=st, d=d, nm=nm, kin=kin, hdx=hdx: e.dma_start(
                                     out=st[:, 0:256], in_=T[nm][d, hdx, kin * 128:(kin + 1) * 128, :]), dma=True)
                            P.op("pool", [("stage", si)], [("wgt", d, gt, kin)],
                                 lambda e, st=st, d=d, gt=gt, kin=kin: e.tensor_copy(out=wgt[:, d, gt, kin, :], in_=st[:, 0:256]))
                for m in range(2):
                    ft = hdx * 2 + m
                    for tt in range(8):
                        pbk, pk = next_pb()
                        mm_acc(pbk[:], pk, [(wxb[:, k, m * 128:(m + 1) * 128], unT[:, k, tt * 512:(tt + 1) * 512],
                                             [("wxb", k), ("unT", tt)]) for k in range(8)])
                        P.op("dve", [pk], [("xb", tt)],
                             lambda e, pbk=pbk, tt=tt: e.tensor_copy(out=xb[:, 2 + tt * 512: 2 + (tt + 1) * 512], in_=pbk[:]))
                    P.op("act", ["xb", "lsm"], [("xc", m)],
                         lambda e, m=m, ft=ft: e.activation(out=xc[:, m, :], in_=xb[:, 0:S], func=AF.Identity,
                                                            scale=lsm[:, ft, 0:1], bias=lsm[:, ft, 4:5]))
                    for k in range(1, 4):
                        P.op("dve", ["xb", ("xc", m), "lsm"], [("xc", m)],
                             lambda e, m=m, ft=ft, k=k: e.scalar_tensor_tensor(
                                 out=xc[:, m, :], in0=xb[:, k:k + S], scalar=lsm[:, ft, k:k + 1], in1=xc[:, m, :],
                                 op0=ALU.mult, op1=ALU.add))
                    P.op("act", [("xc", m)], [("xcb", m)], lambda e, m=m: e.copy(out=xcb[:, m, :], in_=xc[:, m, :]))
                for m in range(2):
                    ft = hdx * 2 + m
                    for d in range(2):
                        for q in (range(4) if d == 0 else range(3, -1, -1)):
                            qs = slice(q * Q, (q + 1) * Q)
                            for t2 in range(2):
                                sl = slice(q * Q + t2 * 512, q * Q + (t2 + 1) * 512)
                                ls_ = slice(t2 * 512, (t2 + 1) * 512)
                                pa, pka = next_pb()
                                mm_acc(pa[:], pka, [(wgt[:, d, 0, kin, m * 128:(m + 1) * 128], xcb[:, kin, sl],
                                                     [("wgt", d, 0, kin), ("xcb", kin)]) for kin in range(2)])
                                P.op("act", [pka, "lsm"], [("ra", t2)],
                                     lambda e, pa=pa, ls_=ls_, ft=ft, d=d: e.activation(out=ra[:, ls_], in_=pa[:], func=AF.Sigmoid,
                                                                                       bias=lsm[:, ft, 5 + d:6 + d]))
                                px, pkx = next_pb()
                                mm_acc(px[:], pkx, [(wgt[:, d, 1, kin, m * 128:(m + 1) * 128], xcb[:, kin, sl],
                                                     [("wgt", d, 1, kin), ("xcb", kin)]) for kin in range(2)])
                                P.op("act", [pkx, "lsm"], [("ri", t2)],
                                     lambda e, px=px, ls_=ls_, ft=ft, d=d: e.activation(out=ri[:, ls_], in_=px[:], func=AF.Sigmoid,
                                                                                       bias=lsm[:, ft, 7 + d:8 + d]))
                            P.op("act", ["ra", "cn2"], ["rm"],
                                 lambda e, ft=ft, d=d: e.activation(out=rm[:], in_=ra[:], func=AF.Exp, scale=cn2[:, ft, d:d + 1]))
                            P.op("act", ["rm", "ones"], ["rm"],
                                 lambda e: e.activation(out=rm[:], in_=rm[:], func=AF.Ln, scale=-1.0, bias=ones[:]))
                            P.op("act", ["rm"], ["rm"],
                                 lambda e: e.activation(out=rm[:], in_=rm[:], func=AF.Exp, scale=0.5))
                            P.op("act", ["ra", "cneg"], ["ra"],
                                 lambda e, ft=ft, d=d: e.activation(out=ra[:], in_=ra[:], func=AF.Exp, scale=cneg[:, ft, d:d + 1]))
                            P.op("pool", ["ri", ("xc", m)], ["ri"],
                                 lambda e, m=m, qs=qs: e.tensor_tensor(out=ri[:], in0=ri[:], in1=xc[:, m, qs], op=ALU.mult))
                            P.op("dve", ["ri", "rm"], ["ri"], lambda e: e.tensor_tensor(out=ri[:], in0=ri[:], in1=rm[:], op=ALU.mult))
                            if d == 0:
                                ini = 0.0 if q == 0 else hb[:, q * Q - 1:q * Q]
                                P.op("dve", ["ra", "ri", "hb"], ["hb"],
                                     lambda e, qs=qs, ini=ini: e.tensor_tensor_scan(out=hb[:, qs], data0=ra[:], data1=ri[:], initial=ini,
                                                                                    op0=ALU.mult, op1=ALU.add))
                            else:
                                ini = 0.0 if q == 3 else carry[:, 0:1]
                                P.op("dve", ["ra", "ri", "carry"], ["rm"],
                                     lambda e, ini=ini: e.tensor_tensor_scan(out=rm[:, ::-1], data0=ra[:, ::-1], data1=ri[:, ::-1],
                                                                             initial=ini, op0=ALU.mult, op1=ALU.add))
                                P.op("dve", ["rm"], ["carry"], lambda e: e.tensor_copy(out=carry[:], in_=rm[:, 0:1]))
                                P.op("pool", ["hb", "rm"], ["hb"], lambda e, qs=qs: e.tensor_tensor(out=hb[:, qs], in0=hb[:, qs], in1=rm[:], op=ALU.add))
                    for tt in range(8):
                        sl = slice(tt * 512, (tt + 1) * 512)
                        pg, pkg = next_pb()
                        mm_acc(pg[:], pkg, [(wgb[:, k, m * 128:(m + 1) * 128], unT[:, k, sl], [("wgb", k), ("unT", tt)])
                                            for k in range(8)])
                        P.op("act", [pkg], ["gtm"], lambda e, pg=pg: e.activation(out=gtm[:], in_=pg[:], func=AF.Gelu_apprx_tanh))
                        P.op("dve", ["hb", "gtm"], [("ybt", tt)], lambda e, sl=sl: e.tensor_tensor(out=ybt[:, sl], in0=hb[:, sl], in1=gtm[:], op=ALU.mult))
                    P.op("sp", ["ybt"], [("ybs", ft)], lambda e, ft=ft: e.dma_start(out=ybs[:, ft, :], in_=ybt[:]), dma=True)

    P.mute = False
    P.barrier()
    if KSTOP <= 3:
        P.dead = True
    c34 = ExitStack()
    nc._late = c34
    gaT = sbt(c34, "gaT", [128, 8, S], BF16)
    if True:
        with ExitStack() as c3:
            sp_ = {}
            for nm in ("lamre", "lamim", "logdt"):
                sp_[nm] = sbt(c3, "s5" + nm, [128, 64], F32)
                P.op("sp", [], [nm], lambda e, nm=nm: e.dma_start(out=sp_[nm][:], in_=T["s5_" + nm][:, :]), dma=True)
            dcol = sbt(c3, "dcol", [128, 8], F32)
            P.op("sp", [], ["dcol"], lambda e: e.dma_start(out=dcol[:], in_=T["s5_dcol"][:, :]), dma=True)
            maskf = sbt(c3, "maskf", [128, 128], F32)
            maskb = sbt(c3, "maskb", [128, 128], F32)
            kidx = sbt(c3, "kidx", [128, 512], F32)
            tauin = sbt(c3, "tauin", [128, 8], F32)
            tauout = sbt(c3, "tauout", [128, 8], F32)
            P.op("sp", [], ["maskf"], lambda e: e.dma_start(out=maskf[:], in_=T["maskf"][:, :]), dma=True)
            P.op("sp", [], ["maskb"], lambda e: e.dma_start(out=maskb[:], in_=T["maskb"][:, :]), dma=True)
            P.op("sp", [], ["kidx"], lambda e: e.dma_start(out=kidx[:], in_=T["kidx"][:, :]), dma=True)
            P.op("sp", [], ["tauin"], lambda e: e.dma_start(out=tauin[:], in_=T["tauin"][:, :]), dma=True)
            P.op("sp", [], ["tauout"], lambda e: e.dma_start(out=tauout[:], in_=T["tauout"][:, :]), dma=True)

            def s5t(name, shape, dt=F32):
                return sbt(c3, name, shape, dt)

            dt_ = s5t("dt_", [128, 64])
            ar = s5t("ar", [128, 64])
            ai = s5t("ai", [128, 64])
            P.op("act", ["logdt"], ["dt_"], lambda e: e.activation(out=dt_[:], in_=sp_["logdt"][:], func=AF.Exp))
            P.op("dve", ["dt_", "lamre"], ["ar"], lambda e: e.tensor_tensor(out=ar[:], in0=sp_["lamre"][:], in1=dt_[:], op=ALU.mult))
            P.op("dve", ["dt_", "lamim"], ["ai"], lambda e: e.tensor_tensor(out=ai[:], in0=sp_["lamim"][:], in1=dt_[:], op=ALU.mult))

            F_MAX = 512
            ce_k = s5t("ce_k", [128, F_MAX], I32)
            ce_f = s5t("ce_f", [128, F_MAX])
            ce_r = s5t("ce_r", [128, F_MAX])
            ce_a = s5t("ce_a", [128, F_MAX])
            ce_e = s5t("ce_e", [128, F_MAX])

            def cexp(Aap, Akeys, Map, Mkeys, re, im, okeys, F, shape=None, defer=False):
                def v(t):
                    a = t[:, 0:F]
                    return a if shape is None else a.rearrange(shape[0], **shape[1])
                P.op("dve", Akeys, ["ce_k"], lambda e: e.tensor_scalar(out=v(ce_k), in0=Aap, scalar1=1.0 / TWO_PI, scalar2=None, op0=ALU.mult))
                P.op("dve", ["ce_k"], ["ce_f"], lambda e: e.tensor_copy(out=ce_f[:, 0:F], in_=ce_k[:, 0:F]))
                P.op("dve", ["ce_f"] + Akeys, ["ce_r"],
                     lambda e: e.scalar_tensor_tensor(out=v(ce_r), in0=v(ce_f), scalar=-6.28125, in1=Aap, op0=ALU.mult, op1=ALU.add))
                P.op("dve", ["ce_f", "ce_r"], ["ce_r"],
                     lambda e: e.scalar_tensor_tensor(out=ce_r[:, 0:F], in0=ce_f[:, 0:F], scalar=-(TWO_PI - 6.28125), in1=ce_r[:, 0:F],
                                                      op0=ALU.mult, op1=ALU.add))
                P.op("dve", ["ce_r"], ["ce_r"],
                     lambda e: e.tensor_scalar(out=ce_r[:, 0:F], in0=ce_r[:, 0:F], scalar1=3.14159, scalar2=-3.14159, op0=ALU.min, op1=ALU.max))
                P.op("dve", ["ce_r"], ["ce_a"],
                     lambda e: e.scalar_tensor_tensor(out=ce_a[:, 0:F], in0=ce_r[:, 0:F], scalar=-1.0, in1=ce_r[:, 0:F], op0=ALU.mult, op1=ALU.max))
                if Map is None:
                    def act_part():
                        P.op("act", ["ce_r"], okeys[1:2], lambda e: e.activation(out=im, in_=v(ce_r), func=AF.Sin))
                        P.op("act", ["ce_a", "halfpi"], okeys[0:1],
                             lambda e: e.activation(out=re, in_=v(ce_a), func=AF.Sin, scale=-1.0, bias=halfpi[:]))
                    if defer:
                        return act_part
                    act_part()
                else:
                    P.op("act", ["ce_r"], ["ce_r"], lambda e: e.activation(out=ce_r[:, 0:F], in_=ce_r[:, 0:F], func=AF.Sin))
                    P.op("act", ["ce_a", "halfpi"], ["ce_a"],
                         lambda e: e.activation(out=ce_a[:, 0:F], in_=ce_a[:, 0:F], func=AF.Sin, scale=-1.0, bias=halfpi[:]))
                    P.op("act", Mkeys, ["ce_e"], lambda e: e.activation(out=v(ce_e), in_=Map, func=AF.Exp))
                    P.op("dve", ["ce_e", "ce_a"], okeys[0:1], lambda e: e.tensor_tensor(out=re, in0=v(ce_e), in1=v(ce_a), op=ALU.mult))
                    P.op("dve", ["ce_e", "ce_r"], okeys[1:2], lambda e: e.tensor_tensor(out=im, in0=v(ce_e), in1=v(ce_r), op=ALU.mult))

            lbr = s5t("lbr", [128, 64]); lbi = s5t("lbi", [128, 64])
            cexp(ai[:], ["ai"], ar[:], ["ar"], lbr[:], lbi[:], ["lbr", "lbi"], 64)
            qr = s5t("qr", [128, 64]); qi = s5t("qi", [128, 64])
            t1 = s5t("t1", [128, 64]); t2 = s5t("t2", [128, 64]); t3 = s5t("t3", [128, 64])
            lre, lim = sp_["lamre"], sp_["lamim"]
            P.op("dve", ["lamre"], ["t1"], lambda e: e.tensor_tensor(out=t1[:], in0=lre[:], in1=lre[:], op=ALU.mult))
            P.op("dve", ["lamim"], ["t2"], lambda e: e.tensor_tensor(out=t2[:], in0=lim[:], in1=lim[:], op=ALU.mult))
            P.op("dve", ["t1", "t2"], ["t1"], lambda e: e.tensor_tensor(out=t1[:], in0=t1[:], in1=t2[:], op=ALU.add))
            P.op("dve", ["t1"], ["t3"], lambda e: e.reciprocal(out=t3[:], in_=t1[:]))
            P.op("dve", ["lbr"], ["t1"], lambda e: e.tensor_scalar(out=t1[:], in0=lbr[:], scalar1=-1.0, scalar2=None, op0=ALU.add))
            P.op("dve", ["t1", "lamre"], ["qr"], lambda e: e.tensor_tensor(out=qr[:], in0=t1[:], in1=lre[:], op=ALU.mult))
            P.op("dve", ["lbi", "lamim"], ["t2"], lambda e: e.tensor_tensor(out=t2[:], in0=lbi[:], in1=lim[:], op=ALU.mult))
            P.op("dve", ["qr", "t2"], ["qr"], lambda e: e.tensor_tensor(out=qr[:], in0=qr[:], in1=t2[:], op=ALU.add))
            P.op("dve", ["qr", "t3"], ["qr"], lambda e: e.tensor_tensor(out=qr[:], in0=qr[:], in1=t3[:], op=ALU.mult))
            P.op("dve", ["lbi", "lamre"], ["qi"], lambda e: e.tensor_tensor(out=qi[:], in0=lbi[:], in1=lre[:], op=ALU.mult))
            P.op("dve", ["t1", "lamim"], ["t2"], lambda e: e.tensor_tensor(out=t2[:], in0=t1[:], in1=lim[:], op=ALU.mult))
            P.op("dve", ["qi", "t2"], ["qi"], lambda e: e.tensor_tensor(out=qi[:], in0=qi[:], in1=t2[:], op=ALU.subtract))
            P.op("dve", ["qi", "t3"], ["qi"], lambda e: e.tensor_tensor(out=qi[:], in0=qi[:], in1=t3[:], op=ALU.mult))
            th = s5t("th", [128, 64]); lrho = s5t("lrho", [128, 64]); nl8 = s5t("nl8", [128, 64]); rho = s5t("rho", [128, 64])
            kir = s5t("kir", [128, 64]); kii = s5t("kii", [128, 64])
            P.op("dve", ["ai"], ["th"], lambda e: e.tensor_scalar(out=th[:], in0=ai[:], scalar1=8.0, scalar2=None, op0=ALU.mult))
            P.op("dve", ["ar"], ["lrho"], lambda e: e.tensor_scalar(out=lrho[:], in0=ar[:], scalar1=8.0, scalar2=None, op0=ALU.mult))
            P.op("dve", ["ar"], ["nl8"], lambda e: e.tensor_scalar(out=nl8[:], in0=ar[:], scalar1=-8.0, scalar2=None, op0=ALU.mult))
            P.op("act", ["lrho"], ["rho"], lambda e: e.activation(out=rho[:], in_=lrho[:], func=AF.Exp))
            cexp(th[:], ["th"], nl8[:], ["nl8"], kir[:], kii[:], ["kir", "kii"], 64)
            P.op("dve", ["kii"], ["kii"], lambda e: e.tensor_scalar(out=kii[:], in0=kii[:], scalar1=-1.0, scalar2=None, op0=ALU.mult))

            PA = s5t("PA", [128, 8, 8]); PM = s5t("PM", [128, 8, 8])
            pwr = [s5t("pwr%d" % i, [128, 8, 8]) for i in range(2)]
            pwi = [s5t("pwi%d" % i, [128, 8, 8]) for i in range(2)]
            bre = s5t("bre", [128, 8, 16]); bim = s5t("bim", [128, 8, 16])
            cre = s5t("cre", [128, 8, 16]); cim = s5t("cim", [128, 8, 16])
            bbr = s5t("bbr", [128, 8, 16]); bbi = s5t("bbi", [128, 8, 16]); tb1 = s5t("tb1", [128, 8, 16])
            Gr = [s5t("Gr%d" % i, [128, 8, 128]) for i in range(2)]
            Gi = [s5t("Gi%d" % i, [128, 8, 128]) for i in range(2)]
            GSr = s5t("GSr", [128, 8, 128]); GSi = s5t("GSi", [128, 8, 128])
            GSm = s5t("GSm", [128, 2, 2, 128])
            rowm = s5t("rowm", [128, 2])
            P.op("sp", [], ["rowm"], lambda e: e.dma_start(out=rowm[:], in_=T["rowm"][:, :]), dma=True)
            tg1 = s5t("tg1", [128, 8, 128]); tg2 = s5t("tg2", [128, 8, 128])
            GinT = s5t("GinT", [128, 8, 2, 128], BF16)
            GoutB = s5t("GoutB", [128, 8, 2, 128], BF16)
            T0 = s5t("T0", [128, 8, 128], BF16)
            uaP = s5t("uaP", [128, 8, 512], BF16)
            U = s5t("U", [128, 8, 512], BF16)
            Yf = s5t("Yf", [128, 2, 512], F32)
            yP = s5t("yP", [128, 8, 512], F32)
            ang = s5t("ang", [128, 512]); cosT = s5t("cosT", [128, 512]); sinT = s5t("sinT", [128, 512])
            m1 = s5t("m1", [128, 512]); m2 = s5t("m2", [128, 512]); Vr = s5t("Vr", [128, 512]); Vi = s5t("Vi", [128, 512])
            Sr = s5t("Sr", [128, 512]); Si = s5t("Si", [128, 512])
            Zr = s5t("Zr", [128, 516], BF16); Zi = s5t("Zi", [128, 516], BF16)
            P.op("dve", [], ["Zr"], lambda e: e.memset(Zr[:], 0.0))
            P.op("dve", [], ["Zi"], lambda e: e.memset(Zi[:], 0.0))

            def bc3(ap2, n):
                return ap2.unsqueeze(2).to_broadcast([128, 8, n])

            KS5 = int(os.environ.get("KS5", "8"))
            KS5S = os.environ.get("KS5STOP", "Z")
            if KS5S <= "A":
                P.dead = True
            for ft in range(min(8, KS5)):
                gs = slice(ft * 8, (ft + 1) * 8)
                P.op("sp", [("uas", ft)], ["uaP"], lambda e, ft=ft: e.dma_start(out=uaP[:], in_=uas[:, ft, :, :]), dma=True)
                for g in range(8):
                    for j in range(8):
                        P.op("sp", [("uas", ft)], [("U", g)],
                             lambda e, g=g, j=j, ft=ft: e.dma_start(out=U[j * 16:(j + 1) * 16, g, :], in_=uas[g * 16:(g + 1) * 16, ft, j, :]), dma=True)
                for nm, tl in (("s5_bre", bre), ("s5_bim", bim), ("s5_cre", cre), ("s5_cim", cim)):
                    P.op("sp", [], [nm],
                         lambda e, nm=nm, tl=tl: e.dma_start(out=tl[:], in_=T[nm][:, gs, :]), dma=True)
                kb = ["s5_bre", "s5_bim", "s5_cre", "s5_cim"]
                P.op("dve", [kb[0], "qr"], ["bbr"], lambda e: e.tensor_tensor(out=bbr[:], in0=bre[:], in1=bc3(qr[:, gs], 16), op=ALU.mult))
                P.op("dve", [kb[1], "qi"], ["tb1"], lambda e: e.tensor_tensor(out=tb1[:], in0=bim[:], in1=bc3(qi[:, gs], 16), op=ALU.mult))
                P.op("dve", ["bbr", "tb1"], ["bbr"], lambda e: e.tensor_tensor(out=bbr[:], in0=bbr[:], in1=tb1[:], op=ALU.subtract))
                P.op("dve", [kb[1], "qr"], ["bbi"], lambda e: e.tensor_tensor(out=bbi[:], in0=bim[:], in1=bc3(qr[:, gs], 16), op=ALU.mult))
                P.op("dve", [kb[0], "qi"], ["tb1"], lambda e: e.tensor_tensor(out=tb1[:], in0=bre[:], in1=bc3(qi[:, gs], 16), op=ALU.mult))
                P.op("dve", ["bbi", "tb1"], ["bbi"], lambda e: e.tensor_tensor(out=bbi[:], in0=bbi[:], in1=tb1[:], op=ALU.add))
                for io, tau in ((0, tauin), (1, tauout)):
                    P.op("dve", ["ai", "tau%d" % io], ["PA"],
                         lambda e, tau=tau: e.tensor_tensor(out=PA[:], in0=bc3(ai[:, gs], 8), in1=tau[:].unsqueeze(1).to_broadcast([128, 8, 8]), op=ALU.mult))
                    P.op("dve", ["ar", "tau%d" % io], ["PM"],
                         lambda e, tau=tau: e.tensor_tensor(out=PM[:], in0=bc3(ar[:, gs], 8), in1=tau[:].unsqueeze(1).to_broadcast([128, 8, 8]), op=ALU.mult))
                    cexp(PA[:], ["PA"], PM[:], ["PM"], pwr[io][:], pwi[io][:], [("pwr", io), ("pwi", io)], 64,
                         shape=("p (g t) -> p g t", dict(g=8)))
                for io, (cr_, ci_, kr, ki_) in enumerate(((bbr, bbi, "bbr", "bbi"), (cre, cim, kb[2], kb[3]))):
                    def v4(t):
                        return t[:].rearrange("p g (t h) -> p g t h", t=8)
                    pr4 = lambda io=io: pwr[io][:].unsqueeze(3).to_broadcast([128, 8, 8, 16])
                    pi4 = lambda io=io: pwi[io][:].unsqueeze(3).to_broadcast([128, 8, 8, 16])
                    c_r = lambda t=cr_: t[:].unsqueeze(2).to_broadcast([128, 8, 8, 16])
                    c_i = lambda t=ci_: t[:].unsqueeze(2).to_broadcast([128, 8, 8, 16])
                    P.op("dve", [("pwr", io), kr], [("Gr", io)], lambda e, io=io, pr4=pr4, c_r=c_r: e.tensor_tensor(out=v4(Gr[io]), in0=pr4(), in1=c_r(), op=ALU.mult))
                    P.op("pool", [("pwi", io), ki_], ["tg1"], lambda e, pi4=pi4, c_i=c_i: e.tensor_tensor(out=v4(tg1), in0=pi4(), in1=c_i(), op=ALU.mult))
                    P.op("dve", [("Gr", io), "tg1"], [("Gr", io)], lambda e, io=io: e.tensor_tensor(out=Gr[io][:], in0=Gr[io][:], in1=tg1[:], op=ALU.subtract))
                    P.op("dve", [("pwr", io), ki_], [("Gi", io)], lambda e, io=io, pr4=pr4, c_i=c_i: e.tensor_tensor(out=v4(Gi[io]), in0=pr4(), in1=c_i(), op=ALU.mult))
                    P.op("pool", [("pwi", io), kr], ["tg2"], lambda e, pi4=pi4, c_r=c_r: e.tensor_tensor(out=v4(tg2), in0=pi4(), in1=c_r(), op=ALU.mult))
                    if io == 0:
                        P.op("dve", [("Gi", 0), "tg2"], [("Gi", 0)], lambda e: e.tensor_tensor(out=Gi[0][:], in0=Gi[0][:], in1=tg2[:], op=ALU.add))
                    else:
                        P.op("dve", [("Gi", 1), "tg2"], [("Gi", 1)],
                             lambda e: e.scalar_tensor_tensor(out=Gi[1][:], in0=Gi[1][:], scalar=-1.0, in1=tg2[:], op0=ALU.mult, op1=ALU.subtract))
                P.op("dve", [("Gr", 0), "kir"], ["GSr"], lambda e: e.tensor_tensor(out=GSr[:], in0=Gr[0][:], in1=bc3(kir[:, gs], 128), op=ALU.mult))
                P.op("pool", [("Gi", 0), "kii"], ["tg1"], lambda e: e.tensor_tensor(out=tg1[:], in0=Gi[0][:], in1=bc3(kii[:, gs], 128), op=ALU.mult))
                P.op("dve", ["GSr", "tg1"], ["GSr"], lambda e: e.tensor_tensor(out=GSr[:], in0=GSr[:], in1=tg1[:], op=ALU.subtract))
                P.op("dve", [("Gi", 0), "kir"], ["GSi"], lambda e: e.tensor_tensor(out=GSi[:], in0=Gi[0][:], in1=bc3(kir[:, gs], 128), op=ALU.mult))
                P.op("pool", [("Gr", 0), "kii"], ["tg2"], lambda e: e.tensor_tensor(out=tg2[:], in0=Gr[0][:], in1=bc3(kii[:, gs], 128), op=ALU.mult))
                P.op("dve", ["GSi", "tg2"], ["GSi"], lambda e: e.tensor_tensor(out=GSi[:], in0=GSi[:], in1=tg2[:], op=ALU.add))
                P.op("act", [("Gr", 1)], [("GoutB", 0)], lambda e: e.copy(out=GoutB[:, :, 0, :], in_=Gr[1][:]))
                P.op("act", [("Gi", 1)], [("GoutB", 1)], lambda e: e.copy(out=GoutB[:, :, 1, :], in_=Gi[1][:]))
                if KS5S <= "B":
                    P.dead = True
                for g in range(min(8, KS5)):
                    gg = ft * 8 + g
                    P.op("dve", ["kidx", "th"], ["ang"],
                         lambda e, gg=gg: e.tensor_scalar(out=ang[:], in0=kidx[:], scalar1=th[:, gg:gg + 1], scalar2=None, op0=ALU.mult))
                    tab_act = cexp(ang[:], ["ang"], None, [], cosT[:], sinT[:], ["cosT", "sinT"], 512, defer=True)
                    pq, pkq = next_pb()
                    if os.environ.get("KNO_TR"):
                        P.mute = True
                    P.op("pe", [("Gr", 0), "identf"], [pkq], lambda e, pq=pq, g=g: e.transpose(out=pq[:, 0:128], in_=Gr[0][:, g, :], identity=identf[:]))
                    P.op("pe", [("Gi", 0), "identf"], [pkq], lambda e, pq=pq, g=g: e.transpose(out=pq[:, 128:256], in_=Gi[0][:, g, :], identity=identf[:]))
                    P.op("act", [pkq], [("GinT", g)],
                         lambda e, pq=pq, g=g: e.copy(out=GinT[:, g, :, :], in_=pq[:, 0:256].rearrange("p (r n) -> p r n", r=2)))
                    P.mute = False
                    if os.environ.get("KNO_T0"):
                        P.mute = True
                    pf, pkf = next_pb()
                    for hh in range(2):
                        P.op("act", ["GSr", "rowm"], [("GSm", hh, 0)], lambda e, hh=hh, g=g: e.activation(out=GSm[:, hh, 0, :], in_=GSr[:, g, :], func=AF.Copy, scale=rowm[:, hh:hh + 1]))
                        P.op("act", ["GSi", "rowm"], [("GSm", hh, 1)], lambda e, hh=hh, g=g: e.activation(out=GSm[:, hh, 1, :], in_=GSi[:, g, :], func=AF.Copy, scale=rowm[:, hh:hh + 1]))
                    for hh in range(2):
                        mm_acc(pf[:, hh * 128:(hh + 1) * 128], pkf,
                               [(GSm[:, hh, 0, :], Gr[1][:, g, :], [("GSm", hh, 0), ("Gr", 1)]),
                                (GSm[:, hh, 1, :], Gi[1][:, g, :], [("GSm", hh, 1), ("Gi", 1)])])
                    P.op("dve", [pkf, "maskf"], ["m1"], lambda e, pf=pf: e.tensor_tensor(out=m1[:, 0:128], in0=pf[:, 0:128], in1=maskf[:], op=ALU.mult))
                    P.op("dve", [pkf, "maskb"], ["m2"], lambda e, pf=pf: e.tensor_tensor(out=m2[:, 0:128], in0=pf[:, 128:256], in1=maskb[:], op=ALU.mult))
                    P.op("dve", ["m1", "m2"], [("T0", g)], lambda e, g=g: e.tensor_tensor(out=T0[:, g, :], in0=m1[:, 0:128], in1=m2[:, 0:128], op=ALU.add))
                    P.mute = False
                    if KS5S <= "C":
                        P.dead = True
                    pxr, pkxr = next_pb()
                    pxi, pkxi = next_pb()
                    mm_acc(pxr[:], pkxr, [(GinT[:, g, 0, :], U[:, g, :], [("GinT", g), ("U", g)])])
                    mm_acc(pxi[:], pkxi, [(GinT[:, g, 1, :], U[:, g, :], [("GinT", g), ("U", g)])])
                    tab_act()
                    P.op("dve", [pkxr, "cosT"], ["m1"], lambda e, pxr=pxr: e.tensor_tensor(out=m1[:], in0=pxr[:], in1=cosT[:], op=ALU.mult))
                    P.op("dve", [pkxi, "sinT"], ["m2"], lambda e, pxi=pxi: e.tensor_tensor(out=m2[:], in0=pxi[:], in1=sinT[:], op=ALU.mult))
                    P.op("pool", ["m1", "m2"], ["Vr"], lambda e: e.tensor_tensor(out=Vr[:], in0=m1[:], in1=m2[:], op=ALU.add))
                    P.op("dve", [pkxi, "cosT"], ["m1"], lambda e, pxi=pxi: e.tensor_tensor(out=m1[:], in0=pxi[:], in1=cosT[:], op=ALU.mult))
                    P.op("dve", [pkxr, "sinT"], ["m2"], lambda e, pxr=pxr: e.tensor_tensor(out=m2[:], in0=pxr[:], in1=sinT[:], op=ALU.mult))
                    P.op("pool", ["m1", "m2"], ["Vi"], lambda e: e.tensor_tensor(out=Vi[:], in0=m1[:], in1=m2[:], op=ALU.subtract))
                    if KS5S <= "D":
                        P.dead = True
                    for (Vt, St, vk, sk) in ((Vr, Sr, "Vr", "Sr"), (Vi, Si, "Vi", "Si")):
                        P.op("dve", [vk, "rho"], [(sk, 0)],
                             lambda e, Vt=Vt, St=St, gg=gg: e.tensor_tensor_scan(out=St[0:64, :], data0=rho[0:64, gg:gg + 1].to_broadcast([64, 512]),
                                                                                 data1=Vt[0:64, :], initial=0.0, op0=ALU.mult, op1=ALU.add))
                        P.op("dve", [vk, "rho"], [(sk, 1)],
                             lambda e, Vt=Vt, St=St, gg=gg: e.tensor_tensor_scan(out=St[64:128, ::-1], data0=rho[64:128, gg:gg + 1].to_broadcast([64, 512]),
                                                                                 data1=Vt[64:128, ::-1], initial=0.0, op0=ALU.mult, op1=ALU.add))
                    if KS5S <= "E":
                        P.dead = True
                    P.op("dve", ["Sr", "cosT"], ["m1"], lambda e: e.tensor_tensor(out=m1[:], in0=Sr[:], in1=cosT[:], op=ALU.mult))
                    P.op("pool", ["Si", "sinT"], ["m2"], lambda e: e.tensor_tensor(out=m2[:], in0=Si[:], in1=sinT[:], op=ALU.mult))
                    P.op("dve", ["m1", "m2"], [("Zr", 0)], lambda e: e.tensor_tensor(out=Zr[0:64, 2:514], in0=m1[0:64, :], in1=m2[0:64, :], op=ALU.subtract))
                    P.op("dve", ["m1", "m2"], [("Zr", 1)], lambda e: e.tensor_tensor(out=Zr[64:128, 0:512], in0=m1[64:128, :], in1=m2[64:128, :], op=ALU.subtract))
                    P.op("dve", ["Sr", "sinT"], ["m1"], lambda e: e.tensor_tensor(out=m1[:], in0=Sr[:], in1=sinT[:], op=ALU.mult))
                    P.op("pool", ["Si", "cosT"], ["m2"], lambda e: e.tensor_tensor(out=m2[:], in0=Si[:], in1=cosT[:], op=ALU.mult))
                    P.op("dve", ["m1", "m2"], [("Zi", 0)], lambda e: e.tensor_tensor(out=Zi[0:64, 2:514], in0=m1[0:64, :], in1=m2[0:64, :], op=ALU.add))
                    P.op("dve", ["m1", "m2"], [("Zi", 1)], lambda e: e.tensor_tensor(out=Zi[64:128, 0:512], in0=m1[64:128, :], in1=m2[64:128, :], op=ALU.add))
                    py, pky = next_pb()
                    mm_acc(py[:], pky, [(T0[:, g, :], U[:, g, :], [("T0", g), ("U", g)]),
                                        (GoutB[:, g, 0, :], Zr[:, 1:513], [("GoutB", 0), "Zr"]),
                                        (GoutB[:, g, 1, :], Zi[:, 1:513], [("GoutB", 1), "Zi"])])
                    P.op("act", [pky], [("Yf", g % 2)], lambda e, py=py, g=g: e.copy(out=Yf[:, g % 2, :], in_=py[:]))
                    for i in range(8):
                        P.op("sp", [("Yf", g % 2)], [("yP", g)],
                             lambda e, g=g, i=i: e.dma_start(out=yP[g * 16:(g + 1) * 16, i, :], in_=Yf[i * 16:(i + 1) * 16, g % 2, :]), dma=True)
                if KS5S <= "F":
                    P.dead = True
                P.op("dve", ["yP", "uaP", "dcol"], ["yP"],
                     lambda e, ft=ft: e.scalar_tensor_tensor(out=yP[:], in0=uaP[:], scalar=dcol[:, ft:ft + 1], in1=yP[:], op0=ALU.mult, op1=ALU.add))
                P.op("act", ["yP"], [("gaT", ft)],
                     lambda e, ft=ft: e.activation(out=gaT[:, ft, :].rearrange("p (c j) -> p j c", j=8), in_=yP[:], func=AF.Gelu_apprx_tanh))

    P.barrier()
    if "dbg_ga" in T:
        for ft in range(8):
            P.op("sp", [("gaT", ft)], [("dbg", ft)], lambda e, ft=ft: e.dma_start(out=T["dbg_ga"][:, ft, :], in_=gaT[:, ft, :]), dma=True)
    if KSTOP <= 4:
        P.dead = True
    col2 = sbt(c34, "col2", [128, 8], F32)
    P.op("sp", [], ["col2"], lambda e: e.dma_start(out=col2[:], in_=T["glub"][:, :]), dma=True)
    for sub in range(2):
        P.barrier()
        with ExitStack() as c4:
            w1 = sbt(c4, "w1_%d" % sub, [128, 8, D], BF16)
            w2 = sbt(c4, "w2_%d" % sub, [128, 8, D], BF16)
            w3 = sbt(c4, "w3_%d" % sub, [128, 8, D], BF16)
            if sub == 0:
                load_cast(w1, ("w1",), T["s5_glu_w"], 8, D)
                load_cast(w2, ("w2",), T["w_a_out"], 8, D)
                load_cast(w3, ("w3",), T["w_in"][:, 3 * D:4 * D], 8, D)
            else:
                load_cast(w1, ("w1",), T["w_b_out"], 8, D)
                load_cast(w2, ("w2",), T["w_o"], 8, D)
                load_cast(w3, ("w3",), T["w_in"][:, 4 * D:5 * D], 8, D)
            unt = sbt(c4, "unt%d" % sub, [128, 8, 512], BF16)
            xt4 = sbt(c4, "xt4_%d" % sub, [128, 4, D], F32)
            a2 = sbt(c4, "a2_%d" % sub, [128, 8, 512], BF16)
            sg = sbt(c4, "sg_%d" % sub, [128, 512], F32)
            sg2 = sbt(c4, "sg2_%d" % sub, [128, 512], F32)
            ntmp = mk_ntmp(c4, "p4%d" % sub, 2)
            for tt in range(8):
                sl = slice(tt * 512, (tt + 1) * 512)
                for b4 in range(4):
                    tb = tt * 4 + b4
                    P.op("sp", [], [("xt4", b4)],
                         lambda e, b4=b4, tb=tb: e.dma_start(out=xt4[:, b4, :], in_=xd[tb * 128:(tb + 1) * 128, :]), dma=True)
                    norm_T(xt4[:, b4, :], ("xt4", b4), 0, unt[:, :, b4 * 128:(b4 + 1) * 128], ("unt", b4), ntmp)
                if sub == 0:
                    for m in range(8):
                        pz, pkz = next_pb()
                        mm_acc(pz[:], pkz, [(w1[:, k, m * 128:(m + 1) * 128], gaT[:, k, sl], [("w1", k), ("gaT", k)]) for k in range(8)])
                        P.op("act", [pkz, "col2"], ["sg"], lambda e, pz=pz, m=m: e.activation(out=sg[:], in_=pz[:], func=AF.Sigmoid, bias=col2[:, m:m + 1]))
                        P.op("dve", ["sg", ("gaT", m)], [("a2", m)], lambda e, m=m: e.tensor_tensor(out=a2[:, m, :], in0=gaT[:, m, sl], in1=sg[:], op=ALU.mult))
                    for m in range(8):
                        pm_, pkm = next_pb()
                        mm_acc(pm_[:], pkm, [(w3[:, k, m * 128:(m + 1) * 128], unt[:, k, :], [("w3", k), "unt"]) for k in range(8)])
                        P.op("act", [pkm], ["sg"], lambda e, pm_=pm_: e.activation(out=sg[:], in_=pm_[:], func=AF.Sigmoid))
                        pa_, pka = next_pb()
                        mm_acc(pa_[:], pka, [(w2[:, k, m * 128:(m + 1) * 128], a2[:, k, :], [("w2", k), ("a2", k)]) for k in range(8)])
                        P.op("dve", [pka, "sg"], [("gaT", m)], lambda e, pa_=pa_, m=m: e.tensor_tensor(out=gaT[:, m, sl], in0=pa_[:], in1=sg[:], op=ALU.mult))
                else:
                    P.op("sp", ["ybs"], ["a2"], lambda e: e.dma_start(out=a2[:], in_=ybs[:, :, sl]), dma=True)
                    mg = unt
                    for m in range(8):
                        pm_, pkm = next_pb()
                        mm_acc(pm_[:], pkm, [(w3[:, k, m * 128:(m + 1) * 128], unt[:, k, :], [("w3", k), "unt"]) for k in range(8)])
                        P.op("act", [pkm], ["sg"], lambda e, pm_=pm_: e.activation(out=sg[:], in_=pm_[:], func=AF.Sigmoid))
                        pa_, pka = next_pb()
                        mm_acc(pa_[:], pka, [(w1[:, k, m * 128:(m + 1) * 128], a2[:, k, :], [("w1", k), "a2"]) for k in range(8)])
                        P.op("dve", [pka, "sg"], ["sg2"], lambda e, pa_=pa_: e.tensor_tensor(out=sg2[:], in0=pa_[:], in1=sg[:], op=ALU.mult))
                        P.op("dve", ["sg2", ("gaT", m)], [("gaT", m)], lambda e, m=m: e.tensor_tensor(out=gaT[:, m, sl], in0=gaT[:, m, sl], in1=sg2[:], op=ALU.add))
                    if "dbg_mg" in T:
                        P.op("sp", ["gaT"], [("dbgm", tt)], lambda e, sl=sl: e.dma_start(out=T["dbg_mg"][:, :, sl], in_=gaT[:, :, sl]), dma=True)
                    for b4 in range(4):
                        tb = tt * 4 + b4
                        for n2 in range(2):
                            po, pko = next_pb()
                            mm_acc(po[:], pko, [(gaT[:, k, tt * 512 + b4 * 128: tt * 512 + (b4 + 1) * 128], w2[:, k, n2 * 512:(n2 + 1) * 512],
                                                 [("w2", k), ("gaT", k)]) for k in range(8)])
                            P.op("dve", [pko, ("xt4", b4)], [("xt4", b4)],
                                 lambda e, po=po, b4=b4, n2=n2: e.tensor_tensor(out=xt4[:, b4, n2 * 512:(n2 + 1) * 512], in0=po[:],
                                                                                in1=xt4[:, b4, n2 * 512:(n2 + 1) * 512], op=ALU.add))
                        P.op("sp", [("xt4", b4)], [("yd", tb)],
                             lambda e, b4=b4, tb=tb: e.dma_start(out=yd[tb * 128:(tb + 1) * 128, :], in_=xt4[:, b4, :]), dma=True)

    P.barrier()
    c34.close()
    if KSTOP <= 5:
        P.dead = True
    with ExitStack() as c5:
        wg = sbt(c5, "wg", [128, 8, FH], BF16)
        wu = sbt(c5, "wu", [128, 8, FH], BF16)
        wd = sbt(c5, "wd", [128, 22, D], BF16)
        load_cast(wg, ("wg",), T["w_ff_gate"], 8, FH)
        load_cast(wu, ("wu",), T["w_ff_up"], 8, FH)
        load_cast(wd, ("wd",), T["w_ff_down"], 22, D)
        vT = sbt(c5, "vT", [128, 8, 512], BF16)
        h4 = sbt(c5, "h4", [128, 4, D], F32)
        hT = sbt(c5, "hT", [128, 22, 512], BF16)
        sgf = sbt(c5, "sgf", [128, 512], F32)
        ntmp = mk_ntmp(c5, "p5", 1)
        for tt in range(8):
            for b4 in range(4):
                tb = tt * 4 + b4
                P.op("sp", [("yd", tb)], [("h4", b4)],
                     lambda e, b4=b4, tb=tb: e.dma_start(out=h4[:, b4, :], in_=yd[tb * 128:(tb + 1) * 128, :]), dma=True)
                norm_T(h4[:, b4, :], ("h4", b4), 1, vT[:, :, b4 * 128:(b4 + 1) * 128], ("vT", b4), ntmp)
            for m in range(22):
                pg_, pkg = next_pb()
                mm_acc(pg_[:], pkg, [(wg[:, k, m * 128:(m + 1) * 128], vT[:, k, :], [("wg", k), "vT"]) for k in range(8)])
                P.op("act", [pkg], ["sgf"], lambda e, pg_=pg_: e.activation(out=sgf[:], in_=pg_[:], func=AF.Silu))
                pu_, pku = next_pb()
                mm_acc(pu_[:], pku, [(wu[:, k, m * 128:(m + 1) * 128], vT[:, k, :], [("wu", k), "vT"]) for k in range(8)])
                P.op("dve", [pku, "sgf"], [("hT", m)], lambda e, pu_=pu_, m=m: e.tensor_tensor(out=hT[:, m, :], in0=pu_[:], in1=sgf[:], op=ALU.mult))
            for b4 in range(4):
                tb = tt * 4 + b4
                for n2 in range(2):
                    po, pko = next_pb()
                    mm_acc(po[:], pko, [(hT[:, m, b4 * 128:(b4 + 1) * 128], wd[:, m, n2 * 512:(n2 + 1) * 512], [("wd", m), ("hT", m)])
                                        for m in range(22)])
                    P.op("dve", [pko, ("h4", b4)], [("h4", b4)],
                         lambda e, po=po, b4=b4, n2=n2: e.tensor_tensor(out=h4[:, b4, n2 * 512:(n2 + 1) * 512], in0=po[:],
                                                                        in1=h4[:, b4, n2 * 512:(n2 + 1) * 512], op=ALU.add))
                P.op("sp", [("h4", b4)], [("yd", tb)],
                     lambda e, b4=b4, tb=tb: e.dma_start(out=yd[tb * 128:(tb + 1) * 128, :], in_=h4[:, b4, :]), dma=True)

    outs = []
    P.barrier()
    with ExitStack() as c6:
        wpg = sbt(c6, "wpg", [128, 8, D], BF16)
        wpp = sbt(c6, "wpp", [128, 2, D], BF16)
        load_cast(wpg, ("wpg",), T["w_ple_gate"], 8, D)
        load_cast(wpp, ("wpp",), T["w_ple_proj"], 2, D)
        gfin = sbt(c6, "gfin", [128, D], F32)
        P.op("sp", [], ["gfin"], lambda e: e.dma_start(out=gfin[:], in_=T["gfin"][:, :]), dma=True)
        h6 = [sbt(c6, "h6_%d" % i, [128, D], F32) for i in range(2)]
        pt_ = [sbt(c6, "pt_%d" % i, [128, 256], F32) for i in range(2)]
        ptb2 = [sbt(c6, "ptb%d" % i, [128, 256], BF16) for i in range(2)]
        pT2 = [sbt(c6, "pTt%d" % i, [128, 2, 128], BF16) for i in range(2)]
        wT2 = [sbt(c6, "wTt%d" % i, [128, 8, 128], BF16) for i in range(2)]
        gp2 = [sbt(c6, "gp%d" % i, [128, 512], F32) for i in range(4)]
        ob = [sbt(c6, "ob%d" % i, [128, D], F32) for i in range(2)]
        ntmp = mk_ntmp(c6, "p6", 2)
        fsets = mk_ntmp(c6, "p6f", 2)
        for tb in range(32):
            h = h6[tb % 2]
            pp = pt_[tb % 2]
            o_ = ob[tb % 2]
            ptb = ptb2[tb % 2]; pT = pT2[tb % 2]; wT = wT2[tb % 2]; bi = tb % 2
            P.op("sp", [("yd", tb)], [("h6", tb % 2)], lambda e, h=h, tb=tb: e.dma_start(out=h[:], in_=yd[tb * 128:(tb + 1) * 128, :]), dma=True)
            P.op("sp", [], [("pt_", tb % 2)], lambda e, pp=pp, tb=tb: e.dma_start(out=pp[:], in_=T["p"][tb * 128:(tb + 1) * 128, :]), dma=True)
            norm_T(h[:], ("h6", tb % 2), 2, wT[:], ("wT", bi), ntmp)
            P.op("dve", [("pt_", tb % 2)], [("ptb", bi)], lambda e, pp=pp, ptb=ptb: e.tensor_copy(out=ptb[:], in_=pp[:]))
            ptile, pkey = next_pt()
            for k in range(2):
                P.op("pe", [("ptb", bi), "ident"], [pkey],
                     lambda e, k=k, ptile=ptile, ptb=ptb: e.transpose(out=ptile[:, k * 128:(k + 1) * 128], in_=ptb[:, k * 128:(k + 1) * 128], identity=ident[:]))
            P.op("dve", [pkey], [("pTt", bi)], lambda e, ptile=ptile, pT=pT: e.tensor_copy(out=pT[:], in_=ptile[:, 0:256].rearrange("p (k t) -> p k t", k=2)))
            for n2 in range(2):
                nsl = slice(n2 * 512, (n2 + 1) * 512)
                gi_ = (tb % 2) * 2 + n2
                gp = gp2[gi_]
                pg_, pkg = next_pb()
                mm_acc(pg_[:], pkg, [(wT[:, k, :], wpg[:, k, nsl], [("wpg", k), ("wT", bi)]) for k in range(8)])
                P.op("act", [pkg], [("gp", gi_)], lambda e, pg_=pg_, gp=gp: e.activation(out=gp[:], in_=pg_[:], func=AF.Sigmoid))
                pq_, pkq = next_pb()
                mm_acc(pq_[:], pkq, [(pT[:, k, :], wpp[:, k, nsl], [("wpp", k), ("pTt", bi)]) for k in range(2)])
                P.op("dve", [pkq, ("gp", gi_)], [("gp", gi_)], lambda e, pq_=pq_, gp=gp: e.tensor_tensor(out=gp[:], in0=pq_[:], in1=gp[:], op=ALU.mult))
                P.op("dve", [("gp", gi_), ("h6", tb % 2)], [("h6", tb % 2)],
                     lambda e, h=h, nsl=nsl, gp=gp: e.tensor_tensor(out=h[:, nsl], in0=h[:, nsl], in1=gp[:], op=ALU.add))
            fi, junk, ss, rs, rr, xn = fsets[tb % 2]
            P.op("act", [("h6", tb % 2)], [("fjunk", fi)], lambda e, h=h, junk=junk: e.activation(out=junk[:], in_=h[:], func=AF.Square))
            P.op("dve", [("fjunk", fi)], [("fss", fi)], lambda e, junk=junk, ss=ss: e.tensor_reduce(out=ss[:], in_=junk[:], axis=AX.X, op=ALU.add))
            P.op("act", [("fss", fi), "eps"], [("frs", fi)], lambda e, ss=ss, rs=rs: e.activation(out=rs[:], in_=ss[:], func=AF.Sqrt, scale=1.0 / D, bias=eps[:]))
            P.op("dve", [("frs", fi)], [("frr", fi)], lambda e, rs=rs, rr=rr: e.reciprocal(out=rr[:], in_=rs[:]))
            P.op("dve", [("h6", tb % 2), ("frr", fi), "gfin"], [("ob", tb % 2)],
                 lambda e, h=h, o_=o_, rr=rr: e.scalar_tensor_tensor(out=o_[:], in0=h[:], scalar=rr[:, 0:1], in1=gfin[:], op0=ALU.mult, op1=ALU.mult))
            outs.append(P.op("sp", [("ob", tb % 2)], [("yd", tb)],
                             lambda e, o_=o_, tb=tb: e.dma_start(out=yd[tb * 128:(tb + 1) * 128, :], in_=o_[:]), dma=True))
    P.final_wait("sp", outs)


DRAM_SPECS = None


def make_nc(shapes):
    info = None
    for plan in (True, False):
        nc = bass.Bass("TRN2", target_bir_lowering=False)
        T = {}
        for name, (shape, dt) in shapes.items():
            T[name] = nc.dram_tensor(name, list(shape), dt, kind="ExternalInput").ap()
        T["y"] = nc.dram_tensor("y", [S, D], F32, kind="ExternalOutput").ap()
        T["yb_scr"] = nc.dram_tensor("yb_scr", [128, 8, S], BF16, kind="Internal").ap()
        T["ua_scr"] = nc.dram_tensor("ua_scr", [128, 8, 8, 512], BF16, kind="Internal").ap()
        with ExitStack() as ctx, ExitStack() as semctx:
            nc._ctx = ctx
            nc._semctx = semctx
            P = Prog(nc, plan, info)
            try:
                build(nc, P, T)
            except _Stop:
                if getattr(nc, "_late", None) is not None:
                    nc._late.close()
            info = P.signal
    return nc


def host_layout(inp):
    f = lambda a: np.ascontiguousarray(np.asarray(a, dtype=np.float32))
    sh = {}
    sh["w_in"] = f(inp["w_in"][0])
    for k in ("s5_glu_w", "w_a_out", "w_b_out", "w_o", "w_ff_gate", "w_ff_up", "w_ff_down", "w_ple_gate", "w_ple_proj"):
        sh[k] = f(inp[k][0])
    sh["lru_wa"] = f(inp["lru_wa"][0])
    sh["lru_wx"] = f(inp["lru_wx"][0])
    col = lambda v: np.asarray(v, np.float32).reshape(8, 128).T
    sh["gcols"] = f(np.stack([col(inp["g_mix"][0]), col(inp["g_ffn"][0]), col(inp["g_ple"][0])], axis=1))
    sh["glub"] = f(col(inp["s5_glu_b"][0]))
    sh["s5_dcol"] = f(col(inp["s5_d"][0]))
    sh["gfin"] = f(np.broadcast_to(np.asarray(inp["g_final"], np.float32)[None, :], (128, D)))
    ls = np.zeros((128, 8, 16), np.float32)
    for k in range(4):
        ls[:, :, k] = col(inp["lru_conv_w"][0, k])
    ls[:, :, 4] = col(inp["lru_conv_b"][0])
    for d in range(2):
        ls[:, :, 5 + d] = col(inp["lru_ba"][0, d])
        ls[:, :, 7 + d] = col(inp["lru_bx"][0, d])
        ls[:, :, 9 + d] = col(inp["lru_a_logit"][0, d])
    sh["lru_small"] = ls
    t2 = lambda a: f(np.asarray(a, np.float32).transpose(0, 2, 1).reshape(128, 64))
    sh["s5_lamre"] = t2(inp["s5_lambda_re"][0])
    sh["s5_lamim"] = t2(inp["s5_lambda_im"][0])
    sh["s5_logdt"] = f(np.broadcast_to(np.asarray(inp["s5_log_dt"][0], np.float32)[:, None, :], (2, 64, 64)).reshape(128, 64))
    sh["s5_bre"] = f(np.asarray(inp["s5_b_re"][0], np.float32).transpose(0, 2, 1, 3).reshape(128, 64, 16))
    sh["s5_bim"] = f(np.asarray(inp["s5_b_im"][0], np.float32).transpose(0, 2, 1, 3).reshape(128, 64, 16))
    sh["s5_cre"] = f(np.asarray(inp["s5_c_re"][0], np.float32).transpose(0, 3, 1, 2).reshape(128, 64, 16))
    sh["s5_cim"] = f(np.asarray(inp["s5_c_im"][0], np.float32).transpose(0, 3, 1, 2).reshape(128, 64, 16))
    sh["identf"] = np.eye(128, dtype=np.float32)
    jj = np.arange(128) // 16
    sh["maskf"] = (jj[None, :] >= jj[:, None]).astype(np.float32)
    sh["maskb"] = (jj[None, :] <= jj[:, None]).astype(np.float32)
    kidx = np.zeros((128, 512), np.float32)
    kidx[:64] = np.arange(512, dtype=np.float32)[None, :]
    kidx[64:] = (511 - np.arange(512, dtype=np.float32))[None, :]
    sh["kidx"] = kidx
    tin = np.zeros((128, 8), np.float32); tout = np.zeros((128, 8), np.float32)
    tin[:64] = 7 - np.arange(8); tin[64:] = np.arange(8)
    tout[:64] = np.arange(8) + 1; tout[64:] = 8 - np.arange(8)
    rowm = np.zeros((128, 2), np.float32); rowm[:64, 0] = 1.0; rowm[64:, 1] = 1.0
    sh["rowm"] = rowm
    sh["tauin"] = tin
    sh["tauout"] = tout
    return sh


_NC_CACHE = {}


def kernel(**inputs):
    shared = host_layout(inputs)
    x = np.asarray(inputs["x"], np.float32)
    p = np.asarray(inputs["p"], np.float32)[0]
    shapes = {k: (v.shape, F32) for k, v in shared.items()}
    shapes["x"] = ((S, D), F32)
    shapes["p"] = ((S, 256), F32)
    if "nc" not in _NC_CACHE:
        _NC_CACHE["nc"] = make_nc(shapes)
    nc = _NC_CACHE["nc"]
    in_maps = []
    for c in range(NCORE):
        m = dict(shared)
        m["x"] = np.ascontiguousarray(x[c])
        m["p"] = np.ascontiguousarray(p[c])
        in_maps.append(m)
    res = run_bass_kernel_spmd(nc, in_maps, core_ids=list(range(NCORE)))
    return np.stack([np.asarray(r["y"], np.float32) for r in res.results], axis=0)
```
